# Optimizing a Trainium2 kernel written in Bass

```python
import math
import jax, jax.numpy as jnp
from jax import lax
import numpy as np

D_MODEL = 1024
BATCH = 8
SEQ = 2048
DEPTH = 2
DEC_BATCH = 128
DEC_SEQ = 1
PAST_LEN = 16384
PAGE_SIZE = 128

N_MIXERS = 2
N_RET_LAYERS = (DEPTH + 1) // 2
N_MLP_LAYERS = DEPTH // 2
RET_HEADS = 4
RET_DK = D_MODEL // RET_HEADS
RET_DV = 2 * RET_DK
RET_QKW = RET_HEADS * RET_DK
RET_VW = RET_HEADS * RET_DV
RET_CHUNK = 128
ROPE_BASE = 10000.0
MLP_WIDTH = 2 * D_MODEL
MLP_GROUPS = 8
MLP_GW = MLP_WIDTH // MLP_GROUPS
MLP_CHUNK = 128
ALPHA = (2 * DEPTH) ** 0.25
BETA = (8 * DEPTH) ** -0.25
LN_EPS = 1e-5

kernel_name = 'retnet_gmlp_interleaved_step'


def _layer_norm(x, gain, bias):
    xf = x.astype(jnp.float32)
    mu = jnp.mean(xf, -1, keepdims=True)
    var = jnp.mean(jnp.square(xf - mu), -1, keepdims=True)
    y = (xf - mu) * lax.rsqrt(var + LN_EPS) * gain.astype(jnp.float32) + bias.astype(jnp.float32)
    return y.astype(x.dtype)


def _rotary(x, pos):
    half = x.shape[-1] // 2
    inv = ROPE_BASE ** (-jnp.arange(half, dtype=jnp.float32) / half)
    ang = pos.astype(jnp.float32)[:, None] * inv[None, :]
    cos = jnp.cos(ang)[None, :, None, :]
    sin = jnp.sin(ang)[None, :, None, :]
    x1, x2 = x[..., :half], x[..., half:]
    return jnp.concatenate([x1 * cos - x2 * sin, x2 * cos + x1 * sin], axis=-1)


def _log_gamma():
    return jnp.log1p(-jnp.exp2(-5.0 - jnp.arange(RET_HEADS, dtype=jnp.float32)))


def _retention_block(q, k, v, s0, lg):
    L = q.shape[1]
    idx = jnp.arange(L, dtype=jnp.float32)
    diff = idx[:, None] - idx[None, :]
    decay = jnp.where(diff >= 0, jnp.exp(jnp.maximum(diff, 0.0)[None] * lg[:, None, None]), 0.0)
    scores = jnp.einsum('bihd,bjhd->bhij', q, k) * decay[None]
    o = jnp.einsum('bhij,bjhe->bihe', scores, v)
    cross = jnp.exp((idx[:, None] + 1.0) * lg[None, :])
    o = o + jnp.einsum('bihd,bhde->bihe', q, s0) * cross[None, :, :, None]
    k_dec = k * jnp.exp((L - 1.0 - idx)[:, None] * lg[None, :])[None, :, :, None]
    s1 = jnp.exp(L * lg)[None, :, None, None] * s0 + jnp.einsum('bjhd,bjhe->bhde', k_dec, v)
    return o, s1


def _retention_mixer(x, s0, pos, w_in, gn_gain, w_out):
    B, L, _ = x.shape
    h = x @ w_in
    q, k, v, g = jnp.split(h, [RET_QKW, 2 * RET_QKW, 2 * RET_QKW + RET_VW], axis=-1)
    q = _rotary(q.reshape(B, L, RET_HEADS, RET_DK).astype(jnp.float32), pos)
    k = _rotary(k.reshape(B, L, RET_HEADS, RET_DK).astype(jnp.float32), pos) * (RET_DK ** -0.5)
    v = v.reshape(B, L, RET_HEADS, RET_DV).astype(jnp.float32)
    lg = _log_gamma()
    chunk = min(L, RET_CHUNK)
    nc = L // chunk

    def to_chunks(t):
        return jnp.moveaxis(t.reshape(B, nc, chunk, RET_HEADS, t.shape[-1]), 1, 0)

    def step(s, qkv):
        qc, kc, vc = qkv
        o, s = _retention_block(qc, kc, vc, s, lg)
        return s, o

    s1, o = lax.scan(step, s0.astype(jnp.float32), (to_chunks(q), to_chunks(k), to_chunks(v)))
    o = jnp.moveaxis(o, 0, 1).reshape(B, L, RET_HEADS, RET_DV)
    mu = jnp.mean(o, -1, keepdims=True)
    var = jnp.mean(jnp.square(o - mu), -1, keepdims=True)
    o = ((o - mu) * lax.rsqrt(var + LN_EPS)).reshape(B, L, RET_VW) * gn_gain.astype(jnp.float32)
    y = (jax.nn.silu(g.astype(jnp.float32)) * o).astype(x.dtype) @ w_out
    return y, s1.astype(x.dtype)


def _chunk_mlp_mixer(x, w_in, ln_g, ln_b, w_s, b_s, w_out):
    B, L, _ = x.shape
    chunk = min(L, MLP_CHUNK)
    nc = L // chunk
    u, v, g = jnp.split(x @ w_in, 3, axis=-1)
    u = jax.nn.gelu(u, approximate=False)
    v = _layer_norm(jax.nn.gelu(v, approximate=False), ln_g, ln_b)
    causal = jnp.tril(jnp.ones((chunk, chunk), dtype=bool))
    ws = jnp.where(causal[None], w_s[:, :chunk, :chunk], 0.0)
    vc = v.reshape(B, nc, chunk, MLP_GROUPS, MLP_GW)
    mixed = jnp.einsum('gij,bcjgd->bcigd', ws, vc) + jnp.transpose(b_s[:, :chunk])[None, None, :, :, None]
    y = u * mixed.reshape(B, L, MLP_WIDTH).astype(x.dtype) * jax.nn.silu(g)
    return y @ w_out, v


def setup_inputs(seed: int = 0) -> dict:
    key = jax.random.key(seed)
    ks = jax.random.split(key, 16)
    f32 = jnp.float32
    x_prompt = jax.random.normal(ks[0], (BATCH, SEQ, D_MODEL), f32)
    x_sample = jax.random.normal(ks[1], (DEC_BATCH, DEC_SEQ, D_MODEL), f32)
    state_ret = 0.05 * jax.random.normal(ks[2], (N_RET_LAYERS, DEC_BATCH, RET_HEADS, RET_DK, RET_DV), f32)
    ln_gain = 1.0 + 0.01 * jax.random.normal(ks[3], (DEPTH, D_MODEL), f32)
    ln_bias = 0.01 * jax.random.normal(ks[4], (DEPTH, D_MODEL), f32)
    ret_scale = jnp.concatenate([
        jnp.full((2 * RET_QKW,), D_MODEL ** -0.5, f32),
        jnp.full((RET_VW,), BETA * D_MODEL ** -0.5, f32),
        jnp.full((RET_VW,), D_MODEL ** -0.5, f32)])
    w_in_ret = jax.random.normal(ks[5], (N_RET_LAYERS, D_MODEL, 2 * RET_QKW + 2 * RET_VW), f32) * ret_scale
    gn_gain_ret = 1.0 + 0.01 * jax.random.normal(ks[6], (N_RET_LAYERS, RET_VW), f32)
    w_out_ret = BETA * RET_VW ** -0.5 * jax.random.normal(ks[7], (N_RET_LAYERS, RET_VW, D_MODEL), f32)
    mlp_scale = jnp.concatenate([
        jnp.full((MLP_WIDTH,), BETA * D_MODEL ** -0.5, f32),
        jnp.full((2 * MLP_WIDTH,), D_MODEL ** -0.5, f32)])
    w_in_mlp = jax.random.normal(ks[8], (N_MLP_LAYERS, D_MODEL, 3 * MLP_WIDTH), f32) * mlp_scale
    ln_gain_mlp = 1.0 + 0.01 * jax.random.normal(ks[9], (N_MLP_LAYERS, MLP_WIDTH), f32)
    ln_bias_mlp = 0.01 * jax.random.normal(ks[10], (N_MLP_LAYERS, MLP_WIDTH), f32)
    w_spatial = MLP_CHUNK ** -0.5 * jax.random.normal(ks[11], (N_MLP_LAYERS, MLP_GROUPS, MLP_CHUNK, MLP_CHUNK), f32)
    b_spatial = 1.0 + 0.01 * jax.random.normal(ks[12], (N_MLP_LAYERS, MLP_GROUPS, MLP_CHUNK), f32)
    w_out_mlp = BETA * MLP_WIDTH ** -0.5 * jax.random.normal(ks[13], (N_MLP_LAYERS, MLP_WIDTH, D_MODEL), f32)
    return {'x_prompt': x_prompt, 'x_sample': x_sample, 'state_ret': state_ret,
            'ln_gain': ln_gain, 'ln_bias': ln_bias,
            'w_in_ret': w_in_ret, 'gn_gain_ret': gn_gain_ret, 'w_out_ret': w_out_ret,
            'w_in_mlp': w_in_mlp, 'ln_gain_mlp': ln_gain_mlp, 'ln_bias_mlp': ln_bias_mlp,
            'w_spatial': w_spatial, 'b_spatial': b_spatial, 'w_out_mlp': w_out_mlp}


def reference(x_prompt, x_sample, state_ret, ln_gain, ln_bias, w_in_ret, gn_gain_ret, w_out_ret,
              w_in_mlp, ln_gain_mlp, ln_bias_mlp, w_spatial, b_spatial, w_out_mlp):
    xp, xs = x_prompt, x_sample
    pos_p = jnp.arange(xp.shape[1], dtype=jnp.int32)
    pos_s = PAST_LEN + jnp.arange(xs.shape[1], dtype=jnp.int32)
    ret_p, ret_s, mlp_s = [], [], []
    for i in range(DEPTH):
        j = i // N_MIXERS
        if i % N_MIXERS == 0:
            s0_p = jnp.zeros((xp.shape[0], RET_HEADS, RET_DK, RET_DV), xp.dtype)
            fp, sp = _retention_mixer(xp, s0_p, pos_p, w_in_ret[j], gn_gain_ret[j], w_out_ret[j])
            fs, ss = _retention_mixer(xs, state_ret[j], pos_s, w_in_ret[j], gn_gain_ret[j], w_out_ret[j])
            ret_p.append(sp)
            ret_s.append(ss)
        else:
            fp, _ = _chunk_mlp_mixer(xp, w_in_mlp[j], ln_gain_mlp[j], ln_bias_mlp[j], w_spatial[j], b_spatial[j], w_out_mlp[j])
            fs, vs = _chunk_mlp_mixer(xs, w_in_mlp[j], ln_gain_mlp[j], ln_bias_mlp[j], w_spatial[j], b_spatial[j], w_out_mlp[j])
            mlp_s.append(vs)
        xp = _layer_norm(ALPHA * xp + fp, ln_gain[i], ln_bias[i])
        xs = _layer_norm(ALPHA * xs + fs, ln_gain[i], ln_bias[i])
    ret_state_prompt = jnp.stack(ret_p)
    ret_state_sample = jnp.stack(ret_s)
    mlp_v_sample = jnp.stack(mlp_s)
    return (xp, xs, ret_state_prompt, ret_state_sample, mlp_v_sample)
```

```python
import math
from contextlib import ExitStack

import numpy as np
import concourse.bass as bass
import concourse.mybir as mybir
from concourse.bass_utils import run_bass_kernel_spmd

F32 = mybir.dt.float32
BF16 = mybir.dt.bfloat16
AF = mybir.ActivationFunctionType
ALU = mybir.AluOpType

T = 2048
D = 1024
H = 4
TB = 512
NB = T // TB
NCH = TB // 128
NS = 16
ALPHA = 4.0 ** 0.25
EPS = 1e-5
PAST = 16384
N_CORES = 8

ENGS = ['pe', 'act', 'dve', 'pool', 'sp']
SAME_ENG_SYNC = {'pe': False, 'act': True, 'dve': True, 'pool': True, 'sp': False}


class Res:
    __slots__ = ('name', 'w', 'r', 'dsem', 'dcnt')

    def __init__(self, name):
        self.name = name
        self.w = []
        self.r = []
        self.dsem = {}
        self.dcnt = {}


class FW:
    def __init__(self, nc):
        self.nc = nc
        self.q = {e: [] for e in ENGS}
        self.cnt = {e: 0 for e in ENGS}
        self.pending = {e: [] for e in ENGS}
        self.waited = {e: {} for e in ENGS}
        self.bar = {e: {} for e in ENGS}
        self.dres = []
        self.final = []

    def _collect(self, eng, reads, writes):
        need = {}

        def add(tok, what):
            key, val, teng = tok
            if val is None:
                if teng == eng:
                    return
                raise RuntimeError(f"{eng} op depends on pending token of {teng} ({what})")
            if key[0] == 'e' and teng == eng and not SAME_ENG_SYNC[eng]:
                return
            if self.waited[eng].get(key, 0) >= val:
                return
            if need.get(key, 0) < val:
                need[key] = val

        for r in reads:
            for t in r.w:
                add(t, r.name)
        for w in writes:
            for t in w.w:
                add(t, w.name)
            for t in w.r:
                add(t, w.name)
        for key, val in self.bar[eng].items():
            if self.waited[eng].get(key, 0) < val and need.get(key, 0) < val:
                need[key] = val
        self.bar[eng] = {}
        for k, v in need.items():
            self.waited[eng][k] = v
        return list(need.items())

    def barrier(self):
        ce = ['pe', 'act', 'dve', 'pool']
        for e in ce:
            if self.pending[e]:
                raise RuntimeError("barrier with pending tokens")
        for e in ce + ['sp']:
            for o in ce:
                if o != e and self.cnt[o] > 0:
                    self.bar[e][('e', o)] = self.cnt[o]

    def op(self, eng, meth, *args, reads=(), writes=(), inc=True, **kw):
        fn = (meth, args, kw)
        waits = self._collect(eng, reads, writes)
        if inc:
            self.cnt[eng] += 1
            tok = [('e', eng), self.cnt[eng], eng]
            for p in self.pending[eng]:
                p[0] = tok[0]
                p[1] = tok[1]
            self.pending[eng] = []
        else:
            tok = [None, None, eng]
            self.pending[eng].append(tok)
        for r in reads:
            r.r.append(tok)
        for w in writes:
            w.w = [tok]
            w.r = []
        self.q[eng].append((fn, waits, inc, None))

    def dma(self, eng, out_ap, in_ap, reads=(), writes=(), out=False, partial=False):
        waits = self._collect(eng, reads, () if partial else writes)
        res = writes[0] if writes else reads[0]
        if eng not in res.dsem:
            res.dsem[eng] = len(self.dres)
            self.dres.append(res)
            res.dcnt[eng] = 0
        res.dcnt[eng] += 1
        tok = [('d', res.dsem[eng]), 16 * res.dcnt[eng], eng]
        for r in reads:
            r.r.append(tok)
        for w in writes:
            if partial:
                w.w.append(tok)
            else:
                w.w = [tok]
                w.r = []
        if out:
            self.final.append(tok)
        self.q[eng].append((('dma_start', (), dict(out=out_ap, in_=in_ap)), waits, False, tok))

    def emit(self):
        nc = self.nc
        for e in ENGS:
            if self.pending[e]:
                raise RuntimeError(f"pending tokens left on {e}")
        with ExitStack() as st:
            esem = {e: st.enter_context(nc.semaphore(f"s_{e}")) for e in ENGS}
            dsem = [st.enter_context(nc.semaphore(f"d_{i}")) for i in range(len(self.dres))]
            block = st.enter_context(nc.Block())

            def handle(key):
                return esem[key[1]] if key[0] == 'e' else dsem[key[1]]

            def run(name, e):
                for fn, waits, inc, dtok in self.q[name]:
                    for key, val in waits:
                        e.wait_ge(handle(key), val)
                    ins = getattr(e, fn[0])(*fn[1], **fn[2])
                    if inc:
                        ins.then_inc(esem[name], 1)
                    if dtok is not None:
                        ins.then_inc(handle(dtok[0]), 16)
                if name == 'sp':
                    fin = {}
                    for key, val, _ in self.final:
                        fin[key] = max(fin.get(key, 0), val)
                    for key, val in fin.items():
                        e.wait_ge(handle(key), val)

            @block.tensor
            def _(e):
                run('pe', e)

            @block.scalar
            def _(e):
                run('act', e)

            @block.vector
            def _(e):
                run('dve', e)

            @block.gpsimd
            def _(e):
                run('pool', e)

            @block.sync
            def _(e):
                run('sp', e)


class Ring:
    def __init__(self, items):
        self.items = items
        self.i = 0

    def get(self):
        it = self.items[self.i % len(self.items)]
        self.i += 1
        return it


def _consts():
    f32 = np.float32
    lg = np.log1p(-np.exp2(-5.0 - np.arange(H, dtype=f32))).astype(f32)
    idx = np.arange(128, dtype=f32)
    c = {}
    c['ident'] = np.eye(128, dtype=f32)
    diff = idx[None, :] - idx[:, None]
    mask = np.zeros((128, H, 128), f32)
    for h in range(H):
        mask[:, h, :] = np.where(diff >= 0, np.exp(np.maximum(diff, 0.0) * lg[h]), 0.0) / 16.0
    c['mask16'] = mask.reshape(128, H * 128)
    c['kdec16'] = (np.exp((127.0 - idx)[:, None] * lg[None, :]) / 16.0).astype(f32)
    cross = np.exp((idx[None, :] + 1.0) * lg[:, None]).astype(f32)
    c['crossb'] = np.ascontiguousarray(np.broadcast_to(cross.reshape(1, H * 128), (128, H * 128)))
    half = 128
    inv = (10000.0 ** (-np.arange(half, dtype=f32) / half)).astype(f32)
    pos = np.arange(T, dtype=f32)
    ang = (inv[:, None] * pos[None, :]).astype(f32)
    c['cosT'] = np.cos(ang).astype(f32)
    c['sinT'] = np.sin(ang).astype(f32)
    angs = (f32(PAST) * inv).astype(f32)
    c['cs_s'] = np.ascontiguousarray(np.broadcast_to(np.cos(angs).astype(f32)[None, :], (NS, 128)))
    c['sn_s'] = np.ascontiguousarray(np.broadcast_to(np.sin(angs).astype(f32)[None, :], (NS, 128)))
    c['causal'] = (diff >= 0).astype(f32)
    dm = np.zeros((128, NS, NS), f32)
    dm[:, np.arange(NS), np.arange(NS)] = 1.0
    c['dmask'] = dm.reshape(128, NS * NS)
    c['dk16'] = (np.eye(NS, dtype=f32) / 16.0)
    gl = [float(np.exp(f32(128.0) * lg[h])) for h in range(H)]
    g1 = [float(np.exp(lg[h])) for h in range(H)]
    return c, gl, g1


CONST_SHAPES = {
    'ident': [128, 128], 'mask16': [128, 512], 'kdec16': [128, 4], 'crossb': [128, 512],
    'cosT': [128, T], 'sinT': [128, T], 'cs_s': [NS, 128], 'sn_s': [NS, 128], 'causal': [128, 128],
    'dmask': [128, NS * NS], 'dk16': [NS, NS],
}
IN_SHAPES = {
    'x': [T, D], 'xs': [NS, D], 'st': [NS, H, 256, 512],
    'w_in_ret': [D, 6144], 'w_out_ret': [2048, D], 'w_in_mlp': [D, 6144], 'w_out_mlp': [2048, D],
    'ln_gain': [2, D], 'ln_bias': [2, D], 'gncol': [128, 16], 'glcol': [128, 16], 'blcol': [128, 16],
    'lgm': [1, 2048], 'lbm': [1, 2048], 'wsT': [128, 1024], 'bsp': [1, 1024], 'ws00': [1, 8], 'bs0': [1, 8],
}
OUT_SHAPES = {'y': [T, D], 'ys': [NS, D], 'sp': [H, 256, 512], 'ss': [NS, H, 256, 512], 'mvs': [NS, 2048]}


def build_program(gl, g1, do_sample=True):
    nc = bass.Bass("TRN2", target_bir_lowering=False)
    di = {}
    for n, s in list(IN_SHAPES.items()) + list(CONST_SHAPES.items()):
        di[n] = nc.dram_tensor(n, s, F32, kind="ExternalInput").ap()
    do = {n: nc.dram_tensor(n, s, F32, kind="ExternalOutput").ap() for n, s in OUT_SHAPES.items()}
    NWIN, NWOUT = 24, 16
    wscr_in = nc.dram_tensor("wscr_in", [NWIN, 128, 4096], BF16).ap()
    wscr_out = nc.dram_tensor("wscr_out", [NWOUT, 128, 2048], BF16).ap()
    fw = FW(nc)
    with ExitStack() as es:
        def sb(name, shape, dt):
            return es.enter_context(nc.sbuf_tensor('sb_' + name, shape, dt))

        def ps(name, shape, dt):
            return es.enter_context(nc.psum_tensor('ps_' + name, shape, dt))

        S32 = sb("S32", [128, 8, 512], F32)
        Sbf = sb("Sbf", [128, 8, 512], BF16)
        S_r = [Res(f"S{i}") for i in range(8)]
        Sb_r = [Res(f"Sb{i}") for i in range(8)]
        identb = sb("identb", [128, 128], BF16)
        cosb = sb("cosb", [128, TB], F32)
        sinb = sb("sinb", [128, TB], F32)
        mask = sb("mask", [128, 4, 128], F32)
        kdec = sb("kdec", [128, 4], F32)
        cross = sb("cross", [128, 4, 128], F32)
        lnb = sb("lnb", [128, 4, 1024], F32)
        gncol = sb("gncol", [128, 16], F32)
        glcol = sb("glcol", [128, 16], F32)
        blcol = sb("blcol", [128, 16], F32)
        wsTb = sb("wsTb", [128, 8, 128], BF16)
        cbT = sb("cbT", [128, 16, 128], F32)
        onesb = sb("onesb", [128, 128], BF16)
        mhalf = sb("mhalf", [128, 1], F32)
        R = sb("R", [128, NCH, 1024], F32)
        XT = sb("XT", [128, 8, TB], BF16)
        xb = [sb(f"xb{i}", [128, 1024], BF16) for i in range(2)]
        slab = [sb(f"slab{i}", [128, 4096], BF16) for i in range(4)]
        NTF = 12
        tmpf = [sb(f"tmpf{i}", [128, 512], F32) for i in range(NTF)]
        tmpk = [sb(f"tmpk{i}", [128, 1024], F32) for i in range(4)]
        stt = [sb(f"stt{i}", [128, 4, 6], F32) for i in range(4)]
        mvt = [sb(f"mvt{i}", [128, 4], F32) for i in range(4)]
        arB = sb("arB", [128, 12288], BF16)
        arF = sb("arF", [128, 6144], F32)
        pf = [ps(f"pf{i}", [128, 512], F32) for i in range(6)]
        pb = [ps(f"pb{i}", [128, 1024], BF16) for i in range(2)]

        C_r = Res("consts")
        cos_r = Res("cos")
        sin_r = Res("sin")
        R_r = [Res(f"R{c}") for c in range(NCH)]
        XT_r = [Res(f"XT{c}") for c in range(NCH)]
        xb_ring = Ring([(xb[i], Res(f"xb{i}")) for i in range(2)])
        win_ring = Ring([(slab[i][:].rearrange("p (k n) -> p k n", k=8), Res(f"slab{i}")) for i in range(3)])
        wout_ring = Ring([(slab[3][:, i * 2048:(i + 1) * 2048].rearrange("p (t n) -> p t n", t=4), Res(f"wo{i}")) for i in range(2)])
        tmpf_it = [(tmpf[i], Res(f"tmpf{i}")) for i in range(NTF)]
        pf_it = [(pf[i], Res(f"pf{i}")) for i in range(6)]
        tmpk_ring = Ring([(tmpk[i], Res(f"tmpk{i}")) for i in range(4)])
        st_ring = Ring([(stt[i], mvt[i], Res(f"st{i}")) for i in range(4)])
        pb_ring = Ring([(pb[i], Res(f"pb{i}")) for i in range(2)])
        tfA, tfB, tfAll = Ring(tmpf_it[0:6]), Ring(tmpf_it[6:12]), Ring(tmpf_it)
        pfA3, pfB3 = Ring(pf_it[0:3]), Ring(pf_it[3:6])
        pfA4, pfB2 = Ring(pf_it[0:4]), Ring(pf_it[4:6])
        pfAll, pf5 = Ring(pf_it), Ring(pf_it[0:5])
        pacc, pacc_r = pf_it[5]

        def v3(ap, a):
            return ap.rearrange("p (a b) -> p a b", a=a)

        wsTf = v3(arF[:, 0:1024], 8)
        bsb = v3(arF[:, 1024:2048], 8)
        causal = arF[:, 2048:2176]
        first = [True]

        def cload(dst, src, eng='sp'):
            fw.dma(eng, dst, src, writes=[C_r], partial=not first[0])
            first[0] = False

        cload(identb[:], di['ident'], 'pool')
        cload(mask[:], v3(di['mask16'], 4))
        cload(kdec[:], di['kdec16'])
        cload(cross[:], v3(di['crossb'], 4))
        for i, (nm, li) in enumerate([('ln_gain', 0), ('ln_bias', 0), ('ln_gain', 1), ('ln_bias', 1)]):
            cload(lnb[:, i, :], di[nm][li, :].partition_broadcast(128))
        cload(gncol[:], di['gncol'])
        cload(glcol[:], di['glcol'])
        cload(blcol[:], di['blcol'])
        H_r = Res("halved")
        fw.op('pool', 'tensor_scalar', glcol[:], glcol[:], 0.5, 1.0, ALU.mult, ALU.mult, reads=[C_r], writes=[H_r])
        M_r = Res("memsets")
        fw.op('pool', 'memset', mhalf[:], -0.5, writes=[M_r], inc=False)
        fw.op('pool', 'memset', onesb[:], 1.0, writes=[M_r])
        setup_r = Res("setup")
        S2_r = Res("setup_in")

        def setup_mlp_tables():
            fw.dma('sp', wsTf, v3(di['wsT'], 8), writes=[S2_r])
            fw.dma('sp', bsb, v3(di['bsp'][0, :].partition_broadcast(128), 8), writes=[S2_r], partial=True)
            fw.dma('sp', causal, di['causal'], writes=[S2_r], partial=True)
            fw.op('dve', 'tensor_tensor', wsTb[:], wsTf, causal.unsqueeze(1).to_broadcast([128, 8, 128]), ALU.mult,
                  reads=[S2_r], writes=[setup_r])
            for half in range(2):
                pt, pr = pfAll.get()
                fw.op('pe', 'matmul', pt[:], onesb[:], wsTb[:, half * 4:(half + 1) * 4, :].rearrange("p a b -> p (a b)"),
                      start=True, stop=True, reads=[setup_r, M_r], writes=[pr])
                for gq in range(4):
                    gi = half * 4 + gq
                    for tt in range(2):
                        ft = gi * 2 + tt
                        fw.op('dve', 'scalar_tensor_tensor', cbT[:, ft, :], pt[:, gq * 128:(gq + 1) * 128], blcol[:, ft:ft + 1],
                              bsb[:, gi, :], ALU.mult, ALU.add, reads=[pr, C_r, S2_r], writes=[setup_r])
            fw.op('dve', 'tensor_scalar', cbT[:], cbT[:], 0.5, None, ALU.mult, reads=[setup_r], writes=[setup_r])

        class WSched:
            def __init__(self, ring, nslots, issue_fn):
                self.plan = []
                self.items = []
                self.issued = 0
                self.released = set()
                self.consumed = 0
                self.ring = ring
                self.ns = nslots
                self.issue_fn = issue_fn

            def issue(self):
                tv, r = self.ring.get()
                self.issue_fn(tv, r, self.plan[self.issued], self.issued)
                self.items.append((tv, r))
                self.issued += 1

            def can_issue(self):
                j = self.issued
                return j < len(self.plan) and (j < self.ns or (j - self.ns) in self.released)

            def pump(self):
                while self.can_issue():
                    self.issue()

            def next(self, *spec):
                assert self.plan[self.consumed][1:] == spec[1:] and self.plan[self.consumed][0] is spec[0], (self.consumed, spec[1:])
                while self.issued <= self.consumed:
                    assert self.can_issue(), ("slab not free", self.consumed)
                    self.issue()
                idx = self.consumed
                self.consumed += 1
                tv, r = self.items[idx]
                return tv, r, idx

            def release(self, *idxs):
                for i in idxs:
                    self.released.add(i)
                self.pump()

        scr_in_r = [Res(f"scr_in{i}") for i in range(NWIN)]
        scr_out_r = [Res(f"scr_out{i}") for i in range(NWOUT)]

        def issue_in(tv, r, spec, j):
            w_ap, pieces = spec
            if j < NWIN:
                for i, (c0, n, off) in enumerate(pieces):
                    fw.dma('pool', tv[:, :, off:off + n], w_ap[:, c0:c0 + n].rearrange("(k p) n -> p k n", p=128),
                           writes=[r], partial=(i > 0))
                fw.dma('sp', wscr_in[j].rearrange("p (k n) -> p k n", k=8), tv, reads=[r], writes=[scr_in_r[j]])
            else:
                jj = j % NWIN
                fw.dma('sp', tv, wscr_in[jj].rearrange("p (k n) -> p k n", k=8), reads=[scr_in_r[jj]], writes=[r])

        def issue_out(tv, r, spec, j):
            w_ap, r0, nh = spec
            if j < NWOUT:
                fw.dma('pool', tv, w_ap[r0:r0 + 512, nh * 512:(nh + 1) * 512].rearrange("(t p) n -> p t n", p=128), writes=[r])
                fw.dma('sp', wscr_out[j].rearrange("p (t n) -> p t n", t=4), tv, reads=[r], writes=[scr_out_r[j]])
            else:
                jj = j % NWOUT
                fw.dma('sp', tv, wscr_out[jj].rearrange("p (t n) -> p t n", t=4), reads=[scr_out_r[jj]], writes=[r])

        Win = WSched(win_ring, 3, issue_in)
        Wout = WSched(wout_ring, 2, issue_out)

        def stats(srcs, np_, src_res):
            stile, mv, r = st_ring.get()
            n = len(srcs)
            for i, s in enumerate(srcs):
                fw.op('dve', 'bn_stats', stile[:np_, i, :], s, reads=src_res, writes=[r], inc=(i == n - 1))
            fw.op('dve', 'bn_aggr', mv[:np_, 0:2], stile[:np_, 0:n, :], reads=[r], writes=[r])
            rstd_chain(mv, r, np_)
            return mv, r

        def rstd_chain(mv, r, np_):
            fw.op('pool', 'tensor_scalar', mv[:np_, 2:3], mv[:np_, 1:2], EPS, None, ALU.add, reads=[r], writes=[r])
            fw.op('pool', 'tensor_tensor', mv[:np_, 2:3], mv[:np_, 2:3], mhalf[:np_, :], ALU.pow, reads=[r, M_r], writes=[r])
            fw.op('pool', 'tensor_scalar', mv[:np_, 3:4], mv[:np_, 0:1], mv[:np_, 2:3], -1.0, ALU.mult, ALU.mult, reads=[r], writes=[r])

        def transposes_to(src, src_res, n, np_, evac):
            pt, pr = pb_ring.get()
            pv = pt[:, 0:n * np_].rearrange("p (a b) -> p a b", a=n)
            for i in range(n):
                fw.op('pe', 'transpose', pv[:, i, :], src[:np_, i * 128:(i + 1) * 128], identb[:np_, :np_],
                      reads=src_res + [C_r], writes=[pr], inc=(i == n - 1))
            evac(pv, pr)

        def layer_norm_chunk(z_ap, z_res, np_, li, out_ap, out_res):
            mv, mr = stats([z_ap[:, 0:512], z_ap[:, 512:1024]], np_, z_res)
            tk, tr = tmpk_ring.get()
            fw.op('act', 'activation', tk[:np_, :], z_ap, AF.Identity, scale=mv[:np_, 2:3], bias=mv[:np_, 3:4],
                  reads=z_res + [mr], writes=[tr])
            fw.op('dve', 'tensor_tensor', tk[:np_, :], tk[:np_, :], lnb[:np_, 2 * li, :], ALU.mult, reads=[tr, C_r], writes=[tr])
            fw.op('dve', 'tensor_tensor', out_ap, tk[:np_, :], lnb[:np_, 2 * li + 1, :], ALU.add, reads=[tr, C_r], writes=out_res)

        ln_active = []

        def gen_ln(z_ap, z_res, li, out_ap, out_res, after):
            stile, mv, r = st_ring.get()
            fw.op('dve', 'bn_stats', stile[:, 0, :], z_ap[:, 0:512], reads=z_res, writes=[r], inc=False)
            fw.op('dve', 'bn_stats', stile[:, 1, :], z_ap[:, 512:1024], reads=z_res, writes=[r])
            fw.op('dve', 'bn_aggr', mv[:, 0:2], stile[:, 0:2, :], reads=[r], writes=[r])
            yield
            rstd_chain(mv, r, 128)
            yield
            tk, tr = tmpk_ring.get()
            fw.op('act', 'activation', tk[:], z_ap, AF.Identity, scale=mv[:, 2:3], bias=mv[:, 3:4], reads=z_res + [r], writes=[tr])
            yield
            fw.op('dve', 'tensor_tensor', tk[:], tk[:], lnb[:, 2 * li, :], ALU.mult, reads=[tr, C_r], writes=[tr])
            fw.op('pool' if li == 1 else 'dve', 'tensor_tensor', out_ap, tk[:], lnb[:, 2 * li + 1, :], ALU.add,
                  reads=[tr, C_r], writes=out_res)
            yield
            after()

        def advance_ln():
            for g in list(ln_active):
                try:
                    next(g)
                except StopIteration:
                    ln_active.remove(g)

        def drain_ln():
            while ln_active:
                advance_ln()

        def accumulate(dst_ap, dst_res, ps_ap, ps_res, first):
            if first:
                fw.op('dve', 'scalar_tensor_tensor', dst_ap, dst_ap, ALPHA, ps_ap, ALU.mult, ALU.add,
                      reads=dst_res + [ps_res], writes=dst_res)
            else:
                fw.op('dve', 'tensor_tensor', dst_ap, dst_ap, ps_ap, ALU.add, reads=dst_res + [ps_res], writes=dst_res)

        def run(g):
            for _ in g:
                pass

        def interleave(a, b):
            alive = [a, b]
            while alive:
                for g in list(alive):
                    try:
                        next(g)
                    except StopIteration:
                        alive.remove(g)

        w_in_ret, w_out_ret, w_in_mlp, w_out_mlp = di['w_in_ret'], di['w_out_ret'], di['w_in_mlp'], di['w_out_mlp']

        def qk_spec(h):
            return (w_in_ret, [(h * 256, 256, 0), (1024 + h * 256, 256, 256)])

        def plan_pass():
            for h in range(H):
                Win.plan.append(qk_spec(h))
                Win.plan.append((w_in_ret, [(2048 + h * 512, 512, 0)]))
                Win.plan.append((w_in_ret, [(4096 + h * 512, 512, 0)]))
                Wout.plan.append((w_out_ret, h * 512, 0))
                Wout.plan.append((w_out_ret, h * 512, 1))
            for s_ in range(4):
                Win.plan.append((w_in_mlp, [(2048 + s_ * 512, 512, 0)]))
            for fg in range(4):
                Win.plan.append((w_in_mlp, [(fg * 512, 512, 0)]))
                Win.plan.append((w_in_mlp, [(4096 + fg * 512, 512, 0)]))
                Wout.plan.append((w_out_mlp, fg * 512, 0))
                Wout.plan.append((w_out_mlp, fg * 512, 1))
        for _ in range(NB + (1 if do_sample else 0)):
            plan_pass()

        def alias_from(dst, src):
            tw = [t for r in src for t in r.w]
            tr = [t for r in src for t in r.r]
            for d in dst:
                d.w = d.w + tw
                d.r = d.r + tr

        qT = [v3(arB[:, (i * 2 + 0) * 1024:(i * 2 + 1) * 1024], 2) for i in range(2)]
        kT = [v3(arB[:, (i * 2 + 1) * 1024:(i * 2 + 2) * 1024], 2) for i in range(2)]
        vh = [v3(arB[:, 4096 + i * 2048:4096 + (i + 1) * 2048], NCH) for i in range(2)]
        gT = [v3(arB[:, 8192 + i * 2048:8192 + (i + 1) * 2048], 4) for i in range(2)]
        sgh = [v3(arF[:, i * 2048:(i + 1) * 2048], NCH) for i in range(2)]
        qT_r = [Res(f"qT{i}") for i in range(2)]
        kT_r = [Res(f"kT{i}") for i in range(2)]
        vh_r = [[Res(f"vh{i}_{c}") for c in range(NCH)] for i in range(2)]
        sg_r = [[Res(f"sg{i}_{c}") for c in range(NCH)] for i in range(2)]
        gT_r = [[Res(f"gT{i}_{c}") for c in range(NCH)] for i in range(2)]
        GV = v3(arB[:, 0:NCH * 2048], NCH)
        GV_r = [Res(f"GV{c}") for c in range(NCH)]
        YT = [v3(arB[:, 8192 + i * 2048:8192 + (i + 1) * 2048], 4) for i in range(2)]
        YT_r = [Res(f"YT{i}") for i in range(2)]
        L0_lo = qT_r + kT_r + [r for l in vh_r for r in l]
        L0_hi = [r for l in gT_r for r in l]
        L0_F = [r for l in sg_r for r in l]

        xb_pref = {}
        ln1_deferred = []

        def prefetch_x(b):
            xb_pref[b] = []
            for c in range(2):
                r0 = b * TB + c * 128
                xt_, xr_ = xb_ring.get()
                fw.dma('pool', xt_[:], di['x'][r0:r0 + 128, :], writes=[xr_])
                xb_pref[b].append((xt_, xr_))

        def gen_loadx(b):
            t0 = b * TB
            fw.dma('sp', cosb[:], di['cosT'][:, t0:t0 + TB], writes=[cos_r])
            fw.dma('sp', sinb[:], di['sinT'][:, t0:t0 + TB], writes=[sin_r])
            for c in range(NCH):
                r0 = t0 + c * 128
                if xb_pref.get(b):
                    xt_, xr_ = xb_pref[b].pop(0)
                else:
                    xt_, xr_ = xb_ring.get()
                    fw.dma('pool', xt_[:], di['x'][r0:r0 + 128, :], writes=[xr_])
                if b == 0 and c == 0:
                    Win.pump()
                    Wout.pump()
                transposes_to(xt_, [xr_], 8, 128,
                              lambda pv, pr, c=c: fw.op('act', 'activation', XT[:, :, c * 128:(c + 1) * 128], pv, AF.Copy,
                                                        reads=[pr], writes=[XT_r[c]]))
                yield

        run(gen_loadx(0))
        for b in range(NB):
            t0 = b * TB
            if b > 0:
                alias_from(L0_lo, GV_r)
                alias_from(L0_hi, YT_r)
            if b == 1:
                alias_from(L0_F, [S2_r])

            def gen_proj(h):
                par = h % 2
                pfP, tfP = (pfAll, tfAll) if h == 0 else (pfB3, tfB)
                sqk, sqk_r, i_qk = Win.next(*qk_spec(h))
                sv, sv_r, i_v = Win.next(w_in_ret, [(2048 + h * 512, 512, 0)])
                sg_, sg_sr, i_g = Win.next(w_in_ret, [(4096 + h * 512, 512, 0)])
                for dstT, dst_r, off in ((qT[par], qT_r[par], 0), (kT[par], kT_r[par], 256)):
                    ts = slice(0, 512)
                    p1, p1r = pfP.get()
                    p2, p2r = pfP.get()
                    for (pt, pr, o2) in ((p1, p1r, off), (p2, p2r, off + 128)):
                        for k in range(8):
                            fw.op('pe', 'matmul', pt[:], sqk[:, k, o2:o2 + 128], XT[:, k, ts], start=(k == 0), stop=(k == 7),
                                  reads=[sqk_r] + XT_r, writes=[pr], inc=(k == 7))
                        yield
                    ta, tar = tfP.get()
                    tb_, tbr = tfP.get()
                    tc, tcr = tfP.get()
                    td, tdr = tfP.get()
                    fw.op('dve', 'tensor_tensor', ta[:], p1[:], cosb[:, ts], ALU.mult, reads=[p1r, cos_r], writes=[tar])
                    fw.op('dve', 'tensor_tensor', tb_[:], p2[:], sinb[:, ts], ALU.mult, reads=[p2r, sin_r], writes=[tbr])
                    fw.op('dve', 'tensor_tensor', tc[:], p2[:], cosb[:, ts], ALU.mult, reads=[p2r, cos_r], writes=[tcr])
                    fw.op('dve', 'tensor_tensor', td[:], p1[:], sinb[:, ts], ALU.mult, reads=[p1r, sin_r], writes=[tdr])
                    fw.op('pool', 'tensor_tensor', dstT[:, 0, ts], ta[:], tb_[:], ALU.subtract, reads=[tar, tbr], writes=[dst_r])
                    fw.op('pool', 'tensor_tensor', dstT[:, 1, ts], tc[:], td[:], ALU.add, reads=[tcr, tdr, dst_r], writes=[dst_r])
                Win.release(i_qk)
                for c in range(NCH):
                    cs = slice(c * 128, (c + 1) * 128)
                    pv_, pvr = pfP.get()
                    for k in range(8):
                        fw.op('pe', 'matmul', pv_[:], XT[:, k, cs], sv[:, k, :], start=(k == 0), stop=(k == 7),
                              reads=[sv_r, XT_r[c]], writes=[pvr], inc=(k == 7))
                    fw.op('act', 'activation', vh[par][:, c, :], pv_[:], AF.Copy, reads=[pvr], writes=[vh_r[par][c]])
                    yield
                    pg_, pgr = pfP.get()
                    for k in range(8):
                        fw.op('pe', 'matmul', pg_[:], XT[:, k, cs], sg_[:, k, :], start=(k == 0), stop=(k == 7),
                              reads=[sg_sr, XT_r[c]], writes=[pgr], inc=(k == 7))
                    fw.op('act', 'activation', sgh[par][:, c, :], pg_[:], AF.Silu, reads=[pgr], writes=[sg_r[par][c]])
                    yield
                Win.release(i_v, i_g)

            xt_pending = {}
            ln0_deferred = []

            def flush_xt(c):
                if c in xt_pending:
                    xt_, xr_ = xt_pending.pop(c)
                    transposes_to(xt_, [xr_], 8, 128,
                                  lambda pv, pr: fw.op('dve', 'tensor_copy', XT[:, :, c * 128:(c + 1) * 128], pv,
                                                       reads=[pr], writes=[XT_r[c]]))

            def gen_ret(h):
                par = h % 2
                wo = [Wout.next(w_out_ret, h * 512, nh) for nh in range(2)]

                def tail(c, gt, gtr):
                    cs = slice(c * 128, (c + 1) * 128)
                    transposes_to(gt, [gtr], 4, 128,
                                  lambda pv, pr: fw.op('dve', 'tensor_tensor', gT[par][:, :, cs], pv,
                                                       gncol[:, h * 4:(h + 1) * 4].unsqueeze(2).to_broadcast([128, 4, 128]), ALU.mult,
                                                       reads=[pr, C_r], writes=[gT_r[par][c]]))
                    yield
                    for nh in range(2):
                        py, pyr = pfA3.get()
                        for t in range(4):
                            fw.op('pe', 'matmul', py[:], gT[par][:, t, cs], wo[nh][0][:, t, :], start=(t == 0), stop=(t == 3),
                                  reads=[gT_r[par][c], wo[nh][1]], writes=[pyr], inc=(t == 3))
                        accumulate(R[:, c, nh * 512:(nh + 1) * 512], [R_r[c]], py[:], pyr, h == 0)
                        yield
                    if h == H - 1:
                        if c >= 1:
                            flush_xt(c - 1)

                        def after(c=c):
                            for c2 in sorted(xt_pending):
                                if c2 <= c - 2:
                                    flush_xt(c2)
                            xt_, xr_ = xb_ring.get()
                            fw.op('act', 'activation', xt_[:], R[:, c, :], AF.Copy, reads=[R_r[c]], writes=[xr_])
                            xt_pending[c] = (xt_, xr_)
                        ln0_deferred.append(gen_ln(R[:, c, :], [R_r[c]], 0, R[:, c, :], [R_r[c]], after))

                pend = None
                for c in range(NCH):
                    gc = b * NCH + c
                    cs = slice(c * 128, (c + 1) * 128)
                    psc, pscr = pfA3.get()
                    for dt in range(2):
                        fw.op('pe', 'matmul', psc[:, 0:128], kT[par][:, dt, cs], qT[par][:, dt, cs], start=(dt == 0), stop=(dt == 1),
                              reads=[kT_r[par], qT_r[par]], writes=[pscr], inc=(dt == 1))
                    sct, sctr = tfA.get()
                    scb = sct[:].bitcast(BF16)[:, 0:128]
                    fw.op('dve', 'tensor_tensor', scb, psc[:, 0:128], mask[:, h, :], ALU.mult, reads=[pscr, C_r], writes=[sctr])
                    kdt, kdr = tfA.get()
                    kd = kdt[:].bitcast(BF16)[:, 0:256]
                    pkt, pkr = pb_ring.get()
                    for dt in range(2):
                        fw.op('pe', 'transpose', pkt[:, dt * 128:(dt + 1) * 128], kT[par][:, dt, cs], identb[:],
                              reads=[kT_r[par], C_r], writes=[pkr], inc=(dt == 1))
                    fw.op('act', 'activation', kd, pkt[:, 0:256], AF.Copy, scale=kdec[:, h:h + 1], reads=[pkr, C_r], writes=[kdr])
                    if gc > 0:
                        qst, qsr = tfA.get()
                        qs = qst[:].bitcast(BF16)[:, 0:256].rearrange("p (a b) -> p a b", a=2)
                        fw.op('pool', 'tensor_tensor', qs, qT[par][:, :, cs], cross[:, h, :].unsqueeze(1).to_broadcast([128, 2, 128]), ALU.mult,
                              reads=[qT_r[par], C_r], writes=[qsr])
                    yield
                    po, por = pfA3.get()
                    fw.op('pe', 'matmul', po[:], scb, vh[par][:, c, :], start=True, stop=(gc == 0),
                          reads=[sctr, vh_r[par][c]], writes=[por], inc=(gc == 0))
                    if gc > 0:
                        for dt in range(2):
                            fw.op('pe', 'matmul', po[:], qs[:, dt, :], Sbf[:, h * 2 + dt, :], start=False, stop=(dt == 1),
                                  reads=[qsr, Sb_r[h * 2 + dt]], writes=[por], inc=(dt == 1))
                    mv, mr = stats([po[:]], 128, [por])
                    on, onr = tfA.get()
                    fw.op('act', 'activation', on[:], po[:], AF.Identity, scale=mv[:, 2:3], bias=mv[:, 3:4], reads=[por, mr], writes=[onr])
                    gtt, gtr = tfA.get()
                    gt = gtt[:].bitcast(BF16)[:, 0:512]
                    fw.op('pool', 'tensor_tensor', gt, on[:], sgh[par][:, c, :], ALU.mult, reads=[onr, sg_r[par][c]], writes=[gtr])
                    yield
                    for dt in range(2):
                        si = h * 2 + dt
                        pu, pur = pfA3.get()
                        fw.op('pe', 'matmul', pu[:], kd[:, dt * 128:(dt + 1) * 128], vh[par][:, c, :], start=True, stop=True,
                              reads=[kdr, vh_r[par][c]], writes=[pur])
                        if gc == 0:
                            fw.op('dve', 'tensor_copy', S32[:, si, :], pu[:], reads=[pur], writes=[S_r[si]])
                        else:
                            fw.op('dve', 'scalar_tensor_tensor', S32[:, si, :], S32[:, si, :], gl[h], pu[:], ALU.mult, ALU.add,
                                  reads=[pur, S_r[si]], writes=[S_r[si]])
                        if gc < NB * NCH - 1:
                            fw.op('act', 'activation', Sbf[:, si, :], S32[:, si, :], AF.Copy, reads=[S_r[si]], writes=[Sb_r[si]])
                    yield
                    if pend is not None:
                        yield from tail(*pend)
                    pend = (c, gt, gtr)
                yield from tail(*pend)
                Wout.release(wo[0][2], wo[1][2])

            for _ in gen_proj(0):
                if ln1_deferred:
                    ln_active.append(ln1_deferred.pop(0))
                advance_ln()
            while ln1_deferred or ln_active:
                if ln1_deferred:
                    ln_active.append(ln1_deferred.pop(0))
                advance_ln()
            for c in range(NCH):
                r0 = t0 + c * 128
                fw.dma('sp', R[:, c, :], di['x'][r0:r0 + 128, :], writes=[R_r[c]])
            for h in range(H):
                if h < H - 1:
                    interleave(gen_ret(h), gen_proj(h + 1))
                else:
                    run(gen_ret(h))
                    while ln0_deferred or ln_active:
                        if ln0_deferred:
                            ln_active.append(ln0_deferred.pop(0))
                        advance_ln()

            if b == NB - 1:
                fw.dma('sp', do['sp'].rearrange("h (dt p) e -> p (h dt) e", p=128), S32[:], reads=S_r, out=True)

            alias_from(GV_r, L0_lo)
            alias_from(YT_r, L0_hi)
            if b == 0:
                alias_from([S2_r], L0_F)
                setup_mlp_tables()
            vst = [st_ring.get() for _ in range(NCH)]
            for s in range(4):
                sl, slr, i_sl = Win.next(w_in_mlp, [(2048 + s * 512, 512, 0)])
                for c in range(NCH):
                    flush_xt(c)
                    cs = slice(c * 128, (c + 1) * 128)
                    pv_, pvr = pfAll.get()
                    for k in range(8):
                        fw.op('pe', 'matmul', pv_[:], XT[:, k, cs], sl[:, k, :], start=(k == 0), stop=(k == 7),
                              reads=[slr, XT_r[c]], writes=[pvr], inc=(k == 7))
                    tg_, tgr = tfAll.get()
                    fw.op('act', 'activation', tg_[:], pv_[:], AF.Gelu, reads=[pvr], writes=[tgr])
                    stile, mv, str_ = vst[c]
                    fw.op('dve', 'bn_stats', stile[:, s, :], tg_[:], reads=[tgr], writes=[str_])
                    fw.op('dve', 'tensor_copy', GV[:, c, s * 512:(s + 1) * 512], tg_[:], reads=[tgr], writes=[GV_r[c]])
                Win.release(i_sl)
            for c in range(NCH):
                stile, mv, r = vst[c]
                fw.op('dve', 'bn_aggr', mv[:, 0:2], stile[:, 0:4, :], reads=[r], writes=[r])
                rstd_chain(mv, r, 128)
                fw.op('act', 'activation', GV[:, c, :], GV[:, c, :], AF.Identity, scale=mv[:, 2:3], bias=mv[:, 3:4],
                      reads=[r, GV_r[c]], writes=[GV_r[c]])

            if b < NB - 1:
                prefetch_x(b + 1)

            def gen_ug(fg):
                par = fg % 2
                su, sur, i_u = Win.next(w_in_mlp, [(fg * 512, 512, 0)])
                sgl, sglr, i_g = Win.next(w_in_mlp, [(4096 + fg * 512, 512, 0)])
                ts = slice(0, 512)
                for t in range(4):
                    ft = fg * 4 + t
                    gi = ft // 2
                    pu, pur = pfA4.get()
                    pg_, pgr = pfA4.get()
                    pm, pmr = pfA4.get()
                    for (pt, pr, sl_, slr_) in ((pu, pur, su, sur), (pg_, pgr, sgl, sglr)):
                        for k in range(8):
                            fw.op('pe', 'matmul', pt[:], sl_[:, k, t * 128:(t + 1) * 128], XT[:, k, ts], start=(k == 0), stop=(k == 7),
                                  reads=[slr_] + XT_r, writes=[pr], inc=(k == 7))
                        yield
                    for cc in range(4):
                        fw.op('pe', 'matmul', pm[:, cc * 128:(cc + 1) * 128], GV[:, cc, ft * 128:(ft + 1) * 128], wsTb[:, gi, :],
                              start=True, stop=True, reads=[GV_r[cc], setup_r], writes=[pmr], inc=(cc == 3))
                    gu, gur = tfAll.get()
                    sgt, sgtr = tfAll.get()
                    mm, mmr = tfAll.get()
                    fw.op('act', 'activation', gu[:], pu[:], AF.Gelu, reads=[pur], writes=[gur])
                    fw.op('act', 'activation', sgt[:], pg_[:], AF.Tanh, scale=0.5, reads=[pgr], writes=[sgtr])
                    fw.op('dve', 'scalar_tensor_tensor', sgt[:], sgt[:], 1.0, pg_[:], ALU.add, ALU.mult, reads=[sgtr, pgr], writes=[sgtr])
                    fw.op('dve', 'scalar_tensor_tensor', v3(mm[:], 4), v3(pm[:], 4), glcol[:, ft:ft + 1],
                          cbT[:, ft, :].unsqueeze(1).to_broadcast([128, 4, 128]), ALU.mult, ALU.add,
                          reads=[pmr, C_r, H_r, setup_r], writes=[mmr])
                    fw.op('dve', 'tensor_tensor', gu[:], gu[:], sgt[:], ALU.mult, reads=[gur, sgtr], writes=[gur])
                    fw.op('pool', 'tensor_tensor', YT[par][:, t, ts], gu[:], mm[:], ALU.mult, reads=[gur, mmr], writes=[YT_r[par]])
                    yield
                Win.release(i_u, i_g)

            def gen_out(fg):
                par = fg % 2
                for nh in range(2):
                    wo, wo_r, i_wo = Wout.next(w_out_mlp, fg * 512, nh)
                    for c in range(NCH):
                        cs = slice(c * 128, (c + 1) * 128)
                        py, pyr = pfB2.get()
                        for t in range(4):
                            fw.op('pe', 'matmul', py[:], YT[par][:, t, cs], wo[:, t, :], start=(t == 0), stop=(t == 3),
                                  reads=[YT_r[par], wo_r], writes=[pyr], inc=(t == 3))
                        accumulate(R[:, c, nh * 512:(nh + 1) * 512], [R_r[c]], py[:], pyr, fg == 0)
                        if fg == 3 and nh == 1:
                            def after(c=c, t0=t0):
                                r0 = t0 + c * 128
                                fw.dma('sp', do['y'][r0:r0 + 128, :], R[:, c, :], reads=[R_r[c]], out=True)
                            ln1_deferred.append(gen_ln(R[:, c, :], [R_r[c]], 1, R[:, c, :], [R_r[c]], after))
                        yield
                    Wout.release(i_wo)

            run(gen_ug(0))
            for fg in range(4):
                g_out = gen_out(fg)
                if fg < 3:
                    g_n = gen_ug(fg + 1)
                    next(g_n)
                    interleave(g_out, g_n)
                elif b < NB - 1:
                    interleave(g_out, gen_loadx(b + 1))
                else:
                    run(g_out)
                    while ln1_deferred or ln_active:
                        if ln1_deferred:
                            ln_active.append(ln1_deferred.pop(0))
                        advance_ln()

        if do_sample:
            fw.barrier()
            Rs = arF[0:NS, 0:1024]
            gvs = arF[0:NS, 1024:3072]
            lgb = arF[0:NS, 3072:4096]
            lbb = arF[0:NS, 4096:5120]
            csn = arF[0:NS, 5120:5376]
            dk16 = arF[0:NS, 5376:5392]
            wb8 = arF[0:NS, 5392:5408]
            o = [0]

            def ab(n, np_=128):
                a = arB[0:np_, o[0]:o[0] + n]
                o[0] += n
                return a
            xsb = ab(1024, NS)
            XTs = v3(ab(8 * NS), 8)
            qkb = ab(512, NS)
            vsb = ab(512, NS)
            kmb = [ab(256, NS) for _ in range(2)]
            qTs = v3(ab(2 * NS), 2)
            qTm = ab(2 * NS * NS).rearrange("p (a b c) -> p a b c", a=2, b=NS)
            s1b = [v3(ab(1024), 2) for _ in range(2)]
            gts = ab(2048, NS)
            gTs = v3(ab(16 * NS), 16)
            ysb = ab(2048, NS)
            yTs = v3(ab(16 * NS), 16)
            dmk = ab(NS * NS).rearrange("p (b c) -> p b c", b=NS)
            A_r = Res("sconst")
            Rs_r = Res("Rs")
            XTs_r = Res("XTs")
            fw.dma('sp', Rs, di['xs'], writes=[Rs_r])
            xsb_r = Res("xsb")
            fw.dma('pool', xsb, di['xs'], writes=[xsb_r])
            fw.dma('sp', csn[:, 0:128], di['cs_s'], writes=[A_r])
            fw.dma('sp', csn[:, 128:256], di['sn_s'], writes=[A_r], partial=True)
            fw.dma('sp', dk16, di['dk16'], writes=[A_r], partial=True)
            fw.dma('sp', wb8[:, 0:8], di['ws00'][0, :].partition_broadcast(NS), writes=[A_r], partial=True)
            fw.dma('sp', wb8[:, 8:16], di['bs0'][0, :].partition_broadcast(NS), writes=[A_r], partial=True)
            fw.dma('pool', dmk, v3(di['dmask'], NS), writes=[A_r], partial=True)
            transposes_to(xsb, [xsb_r], 8, NS,
                          lambda pv, pr: fw.op('dve', 'tensor_copy', XTs, pv, reads=[pr], writes=[XTs_r]))
            cs2 = csn[:, 0:128].unsqueeze(1).to_broadcast([NS, 2, 128])
            sn2 = csn[:, 128:256].unsqueeze(1).to_broadcast([NS, 2, 128])
            gts_r = Res("gts")
            qkb_r, vs_r, qTs_r, qTm_r, gTs_r = Res("qkb"), Res("vs"), Res("qTs"), Res("qTm"), Res("gTs")
            s1b_ring = Ring([(s1b[i], Res(f"s1b{i}")) for i in range(2)])
            km_ring = Ring([(kmb[i], Res(f"km{i}")) for i in range(2)])

            units = [(h, bb) for h in range(H) for bb in range(NS)]
            loaded = []

            def load_unit(u):
                h_, bb_ = units[u]
                stt_, str_ = tmpk_ring.get()
                sv3_ = stt_[:].rearrange("p (a b) -> p a b", a=2)
                fw.dma('sp', sv3_, di['st'][bb_, h_].rearrange("(dt p) e -> p dt e", p=128), writes=[str_])
                loaded.append((sv3_, str_))

            def proj(sl, slr):
                pt, pr = pf5.get()
                for k in range(8):
                    fw.op('pe', 'matmul', pt[0:NS, :], XTs[:, k, :], sl[:, k, :], start=(k == 0), stop=(k == 7),
                          reads=[slr, XTs_r], writes=[pr], inc=(k == 7))
                return pt, pr

            for u in range(3):
                load_unit(u)

            qkb_s = [qkb, ab(512, NS)]
            vsb_s = [vsb, ab(512, NS)]
            qTs_s = [qTs, v3(ab(2 * NS), 2)]
            qTm_s = [qTm, ab(2 * NS * NS).rearrange("p (a b c) -> p a b c", a=2, b=NS)]
            qkb_rs = [qkb_r, Res("qkb1")]
            vs_rs = [vs_r, Res("vs1")]
            qTs_rs = [qTs_r, Res("qTs1")]
            qTm_rs = [qTm_r, Res("qTm1")]
            head = {}

            def pre(h):
                st_ = h % 2
                qkb_, qkbr_, vsb_, vsr_ = qkb_s[st_], qkb_rs[st_], vsb_s[st_], vs_rs[st_]
                sqk, sqk_r, i_qk = Win.next(*qk_spec(h))
                sv, sv_r, i_v = Win.next(w_in_ret, [(2048 + h * 512, 512, 0)])
                sg_, sg_sr, i_g = Win.next(w_in_ret, [(4096 + h * 512, 512, 0)])
                pq, pqr = proj(sqk, sqk_r)
                Win.release(i_qk)
                qk, qkr = tfAll.get()
                fw.op('act', 'activation', qk[0:NS, :], pq[0:NS, :], AF.Copy, reads=[pqr], writes=[qkr])
                qk4 = qk[0:NS, :].rearrange("p (a b c) -> p a b c", a=2, b=2)
                x1, x2 = qk4[:, :, 0, :], qk4[:, :, 1, :]
                tr_, trr = tfAll.get()
                t4 = tr_[0:NS, :].rearrange("p (a b) -> p a b", a=4)
                qkb4 = qkb_.rearrange("p (a b c) -> p a b c", a=2, b=2)
                fw.op('dve', 'tensor_tensor', t4[:, 0:2, :], x1, cs2, ALU.mult, reads=[qkr, A_r], writes=[trr])
                fw.op('dve', 'tensor_tensor', t4[:, 2:4, :], x2, sn2, ALU.mult, reads=[qkr, A_r, trr], writes=[trr])
                fw.op('dve', 'tensor_tensor', qkb4[:, :, 0, :], t4[:, 0:2, :], t4[:, 2:4, :], ALU.subtract, reads=[trr], writes=[qkbr_])
                fw.op('dve', 'tensor_tensor', t4[:, 0:2, :], x2, cs2, ALU.mult, reads=[qkr, A_r, qkbr_], writes=[trr])
                fw.op('dve', 'tensor_tensor', t4[:, 2:4, :], x1, sn2, ALU.mult, reads=[qkr, A_r, trr], writes=[trr])
                fw.op('dve', 'tensor_tensor', qkb4[:, :, 1, :], t4[:, 0:2, :], t4[:, 2:4, :], ALU.add, reads=[trr, qkbr_], writes=[qkbr_])
                pv_, pvr = proj(sv, sv_r)
                Win.release(i_v)
                fw.op('act', 'activation', vsb_, pv_[0:NS, :], AF.Copy, reads=[pvr], writes=[vsr_])
                pg_, pgr = proj(sg_, sg_sr)
                Win.release(i_g)
                sgs, sgsr = tfAll.get()
                fw.op('act', 'activation', sgs[0:NS, :], pg_[0:NS, :], AF.Silu, reads=[pgr], writes=[sgsr])
                transposes_to(qkb_, [qkbr_], 2, NS,
                              lambda pv, pr: fw.op('dve', 'tensor_copy', qTs_s[st_], pv, reads=[pr], writes=[qTs_rs[st_]]))
                fw.op('dve', 'tensor_tensor', qTm_s[st_], qTs_s[st_].unsqueeze(2).to_broadcast([128, 2, NS, NS]),
                      dmk.unsqueeze(1).to_broadcast([128, 2, NS, NS]), ALU.mult, reads=[qTs_rs[st_], A_r], writes=[qTm_rs[st_]])
                head[h] = (sgs, sgsr)

            def post_pe(h):
                woh = [Wout.next(w_out_ret, h * 512, nh) for nh in range(2)]
                transposes_to(gts[:, h * 512:(h + 1) * 512], [gts_r], 4, NS,
                              lambda pv, pr: fw.op('dve', 'tensor_tensor', gTs[:, h * 4:(h + 1) * 4, :], pv,
                                                   gncol[:, h * 4:(h + 1) * 4].unsqueeze(2).to_broadcast([128, 4, NS]), ALU.mult,
                                                   reads=[pr, C_r], writes=[gTs_r]))
                for nh in range(2):
                    py, pyr = pf5.get()
                    for t in range(4):
                        fw.op('pe', 'matmul', py[0:NS, :], gTs[:, h * 4 + t, :], woh[nh][0][:, t, :], start=(t == 0), stop=(t == 3),
                              reads=[gTs_r, woh[nh][1]], writes=[pyr], inc=(t == 3))
                    accumulate(Rs[:, nh * 512:(nh + 1) * 512], [Rs_r], py[0:NS, :], pyr, h == 0)
                Wout.release(woh[0][2], woh[1][2])

            pre(0)
            for h in range(H):
                st_ = h % 2
                qkb_, qkbr_, vsb_, vsr_ = qkb_s[st_], qkb_rs[st_], vsb_s[st_], vs_rs[st_]
                qTm_, qTmr_ = qTm_s[st_], qTm_rs[st_]
                sgs, sgsr = head[h]
                if h + 1 < H:
                    pre(h + 1)
                pos_, posr = pacc, pacc_r
                prev = None
                for bb in range(NS):
                    u = h * NS + bb
                    sv3, str_ = loaded[u]
                    km, kmr = km_ring.get()
                    fw.op('pool', 'tensor_scalar', km, qkb_[:, 256:512], dk16[:, bb:bb + 1], 1.0, ALU.mult, ALU.mult,
                          reads=[qkbr_, A_r], writes=[kmr])
                    s1, s1r = s1b_ring.get()
                    for dt in range(2):
                        pu, pur = pf5.get()
                        fw.op('pe', 'matmul', pu[:], km[:, dt * 128:(dt + 1) * 128], vsb_, start=True, stop=True,
                              reads=[kmr, vsr_], writes=[pur])
                        fw.op('dve', 'scalar_tensor_tensor', sv3[:, dt, :], sv3[:, dt, :], g1[h], pu[:], ALU.mult, ALU.add,
                              reads=[pur, str_], writes=[str_])
                    fw.op('act', 'activation', s1, sv3, AF.Copy, reads=[str_], writes=[s1r])
                    fw.dma('act', do['ss'][bb, h].rearrange("(dt p) e -> p dt e", p=128), sv3, reads=[str_], out=True)
                    if u + 3 < len(units):
                        load_unit(u + 3)
                    for (pbb, ps1, ps1r) in ([prev] if prev is not None else []) + ([(bb, s1, s1r)] if bb == NS - 1 else []):
                        for dt in range(2):
                            fw.op('pe', 'matmul', pos_[0:NS, :], qTm_[:, dt, pbb, :], ps1[:, dt, :],
                                  start=(pbb == 0 and dt == 0), stop=(pbb == NS - 1 and dt == 1),
                                  reads=[qTmr_, ps1r], writes=[posr], inc=True)
                    prev = (bb, s1, s1r)
                    if bb == 5 and h > 0:
                        post_pe(h - 1)
                mv, mr = stats([pos_[0:NS, :]], NS, [posr])
                on, onr = tfAll.get()
                fw.op('act', 'activation', on[0:NS, :], pos_[0:NS, :], AF.Identity, scale=mv[0:NS, 2:3], bias=mv[0:NS, 3:4],
                      reads=[posr, mr], writes=[onr])
                fw.op('pool', 'tensor_tensor', gts[:, h * 512:(h + 1) * 512], on[0:NS, :], sgs[0:NS, :], ALU.mult,
                      reads=[onr, sgsr], writes=[gts_r])
            post_pe(H - 1)
            layer_norm_chunk(Rs, [Rs_r], NS, 0, Rs, [Rs_r])
            fw.op('act', 'activation', xsb, Rs, AF.Copy, reads=[Rs_r], writes=[xsb_r])
            transposes_to(xsb, [xsb_r], 8, NS,
                          lambda pv, pr: fw.op('dve', 'tensor_copy', XTs, pv, reads=[pr], writes=[XTs_r]))
            gvs_r = Res("gvs")
            for s in range(4):
                sl, slr, i_sl = Win.next(w_in_mlp, [(2048 + s * 512, 512, 0)])
                pt, pr = proj(sl, slr)
                Win.release(i_sl)
                fw.op('act', 'activation', gvs[:, s * 512:(s + 1) * 512], pt[0:NS, :], AF.Gelu, reads=[pr, gvs_r], writes=[gvs_r])
            mv, mr = stats([gvs[:, s * 512:(s + 1) * 512] for s in range(4)], NS, [gvs_r])
            fw.op('act', 'activation', gvs, gvs, AF.Identity, scale=mv[0:NS, 2:3], bias=mv[0:NS, 3:4], reads=[gvs_r, mr], writes=[gvs_r])
            lg_r = Res("lgb")
            for hf in range(2):
                fw.dma('sp', lgb, di['lgm'][0, hf * 1024:(hf + 1) * 1024].partition_broadcast(NS), writes=[lg_r])
                fw.dma('sp', lbb, di['lbm'][0, hf * 1024:(hf + 1) * 1024].partition_broadcast(NS), writes=[lg_r], partial=True)
                fw.op('pool', 'tensor_tensor', gvs[:, hf * 1024:(hf + 1) * 1024], gvs[:, hf * 1024:(hf + 1) * 1024], lgb, ALU.mult,
                      reads=[gvs_r, lg_r], writes=[gvs_r])
                fw.op('pool', 'tensor_tensor', gvs[:, hf * 1024:(hf + 1) * 1024], gvs[:, hf * 1024:(hf + 1) * 1024], lbb, ALU.add,
                      reads=[gvs_r, lg_r], writes=[gvs_r])
            fw.dma('sp', do['mvs'], gvs, reads=[gvs_r], out=True)
            mxt, mxr = tmpk_ring.get()
            mxt2, mxr2 = tmpk_ring.get()
            mx3a = mxt[0:NS, :].rearrange("p (g d) -> p g d", g=4)
            mx3b = mxt2[0:NS, :].rearrange("p (g d) -> p g d", g=4)
            gv3 = gvs.rearrange("p (g d) -> p g d", g=8)
            for hf, (m3, mr_) in enumerate(((mx3a, mxr), (mx3b, mxr2))):
                fw.op('dve', 'tensor_tensor', m3, gv3[:, hf * 4:(hf + 1) * 4, :],
                      wb8[:, hf * 4:(hf + 1) * 4].unsqueeze(2).to_broadcast([NS, 4, 256]), ALU.mult, reads=[gvs_r, A_r], writes=[mr_])
                fw.op('dve', 'tensor_tensor', m3, m3, wb8[:, 8 + hf * 4:8 + (hf + 1) * 4].unsqueeze(2).to_broadcast([NS, 4, 256]), ALU.add,
                      reads=[mr_, A_r], writes=[mr_])
            ys_r = Res("ysb")
            yTs_r = Res("yTs")
            for fg in range(4):
                su, sur, i_u = Win.next(w_in_mlp, [(fg * 512, 512, 0)])
                sgl, sglr, i_g = Win.next(w_in_mlp, [(4096 + fg * 512, 512, 0)])
                woh = [Wout.next(w_out_mlp, fg * 512, nh) for nh in range(2)]
                pu, pur = proj(su, sur)
                pg_, pgr = proj(sgl, sglr)
                Win.release(i_u, i_g)
                gu, gur = tfAll.get()
                sgt, sgtr = tfAll.get()
                fw.op('act', 'activation', gu[0:NS, :], pu[0:NS, :], AF.Gelu, reads=[pur], writes=[gur])
                fw.op('act', 'activation', sgt[0:NS, :], pg_[0:NS, :], AF.Silu, reads=[pgr], writes=[sgtr])
                mxs = (mxt if fg < 2 else mxt2)[0:NS, (fg % 2) * 512:(fg % 2 + 1) * 512]
                mxs_r = mxr if fg < 2 else mxr2
                fw.op('pool', 'tensor_tensor', gu[0:NS, :], gu[0:NS, :], sgt[0:NS, :], ALU.mult, reads=[gur, sgtr], writes=[gur])
                fw.op('pool', 'tensor_tensor', ysb[:, fg * 512:(fg + 1) * 512], gu[0:NS, :], mxs, ALU.mult, reads=[gur, mxs_r], writes=[ys_r])
                transposes_to(ysb[:, fg * 512:(fg + 1) * 512], [ys_r], 4, NS,
                              lambda pv, pr: fw.op('dve', 'tensor_copy', yTs[:, fg * 4:(fg + 1) * 4, :], pv, reads=[pr], writes=[yTs_r]))
                for nh in range(2):
                    py, pyr = pf5.get()
                    for t in range(4):
                        fw.op('pe', 'matmul', py[0:NS, :], yTs[:, fg * 4 + t, :], woh[nh][0][:, t, :], start=(t == 0), stop=(t == 3),
                              reads=[yTs_r, woh[nh][1]], writes=[pyr], inc=(t == 3))
                    accumulate(Rs[:, nh * 512:(nh + 1) * 512], [Rs_r], py[0:NS, :], pyr, fg == 0)
                Wout.release(woh[0][2], woh[1][2])
            ot, otr = tmpk_ring.get()
            layer_norm_chunk(Rs, [Rs_r], NS, 1, ot[0:NS, :], [otr])
            fw.dma('sp', do['ys'], ot[0:NS, :], reads=[otr], out=True)

        fw.emit()
    return nc


_CACHE = {}


def kernel(x_prompt, x_sample, state_ret, ln_gain, ln_bias, w_in_ret, gn_gain_ret, w_out_ret,
           w_in_mlp, ln_gain_mlp, ln_bias_mlp, w_spatial, b_spatial, w_out_mlp):
    f32 = np.float32
    A = lambda a: np.ascontiguousarray(np.asarray(a, dtype=f32))
    consts, gl, g1 = _consts()
    if 'nc' not in _CACHE:
        _CACHE['nc'] = build_program(gl, g1)
    nc = _CACHE['nc']
    x_prompt = A(x_prompt)
    x_sample = A(x_sample)
    state_ret = A(state_ret)
    shared = {
        'w_in_ret': A(w_in_ret)[0], 'w_out_ret': A(w_out_ret)[0], 'w_in_mlp': A(w_in_mlp)[0], 'w_out_mlp': A(w_out_mlp)[0],
        'ln_gain': A(ln_gain), 'ln_bias': A(ln_bias),
        'gncol': A(A(gn_gain_ret)[0].reshape(16, 128).T), 'glcol': A(A(ln_gain_mlp)[0].reshape(16, 128).T),
        'blcol': A(A(ln_bias_mlp)[0].reshape(16, 128).T),
        'lgm': A(ln_gain_mlp).reshape(1, 2048), 'lbm': A(ln_bias_mlp).reshape(1, 2048),
        'wsT': A(A(w_spatial)[0].transpose(2, 0, 1).reshape(128, 1024)),
        'bsp': A(b_spatial).reshape(1, 1024),
        'ws00': A(A(w_spatial)[0][:, 0, 0].reshape(1, 8)), 'bs0': A(A(b_spatial)[0][:, 0].reshape(1, 8)),
    }
    shared.update(consts)
    in_maps = []
    for c in range(N_CORES):
        m = dict(shared)
        m['x'] = x_prompt[c]
        m['xs'] = A(x_sample[c * NS:(c + 1) * NS, 0, :])
        m['st'] = A(state_ret[0, c * NS:(c + 1) * NS])
        in_maps.append(m)
    res = run_bass_kernel_spmd(nc, in_maps, core_ids=list(range(N_CORES)))
    rs = res.results
    y_prompt = np.stack([rs[c]['y'] for c in range(N_CORES)], 0).astype(f32)
    y_sample = np.concatenate([rs[c]['ys'] for c in range(N_CORES)], 0).reshape(128, 1, D).astype(f32)
    ret_p = np.stack([rs[c]['sp'] for c in range(N_CORES)], 0)[None].astype(f32)
    ret_s = np.concatenate([rs[c]['ss'] for c in range(N_CORES)], 0)[None].astype(f32)
    mlp_v = np.concatenate([rs[c]['mvs'] for c in range(N_CORES)], 0).reshape(1, 128, 1, 2048).astype(f32)
    return (y_prompt, y_sample, ret_p, ret_s, mlp_v)
```

```python
import math
from contextlib import ExitStack

import numpy as np
import concourse.bass as bass
import concourse.mybir as mybir
from concourse.bass_utils import run_bass_kernel_spmd

F32 = mybir.dt.float32
BF16 = mybir.dt.bfloat16
AF = mybir.ActivationFunctionType
ALU = mybir.AluOpType

T = 2048
D = 1024
H = 4
TB = 512
NB = T // TB
NCH = TB // 128
NS = 16
ALPHA = 4.0 ** 0.25
EPS = 1e-5
PAST = 16384
N_CORES = 8

ENGS = ['pe', 'act', 'dve', 'pool', 'sp']
SAME_ENG_SYNC = {'pe': False, 'act': True, 'dve': True, 'pool': True, 'sp': False}


class Res:
    __slots__ = ('name', 'w', 'r', 'dsem', 'dcnt')

    def __init__(self, name):
        self.name = name
        self.w = []
        self.r = []
        self.dsem = {}
        self.dcnt = {}


class FW:
    def __init__(self, nc):
        self.nc = nc
        self.q = {e: [] for e in ENGS}
        self.cnt = {e: 0 for e in ENGS}
        self.pending = {e: [] for e in ENGS}
        self.waited = {e: {} for e in ENGS}
        self.bar = {e: {} for e in ENGS}
        self.dres = []
        self.final = []

    def _collect(self, eng, reads, writes):
        need = {}

        def add(tok, what):
            key, val, teng = tok
            if val is None:
                if teng == eng:
                    return
                raise RuntimeError(f"{eng} op depends on pending token of {teng} ({what})")
            if key[0] == 'e' and teng == eng and not SAME_ENG_SYNC[eng]:
                return
            if self.waited[eng].get(key, 0) >= val:
                return
            if need.get(key, 0) < val:
                need[key] = val

        for r in reads:
            for t in r.w:
                add(t, r.name)
        for w in writes:
            for t in w.w:
                add(t, w.name)
            for t in w.r:
                add(t, w.name)
        for key, val in self.bar[eng].items():
            if self.waited[eng].get(key, 0) < val and need.get(key, 0) < val:
                need[key] = val
        self.bar[eng] = {}
        for k, v in need.items():
            self.waited[eng][k] = v
        return list(need.items())

    def barrier(self):
        ce = ['pe', 'act', 'dve', 'pool']
        for e in ce:
            if self.pending[e]:
                raise RuntimeError("barrier with pending tokens")
        for e in ce + ['sp']:
            for o in ce:
                if o != e and self.cnt[o] > 0:
                    self.bar[e][('e', o)] = self.cnt[o]

    def op(self, eng, meth, *args, reads=(), writes=(), inc=True, **kw):
        fn = (meth, args, kw)
        waits = self._collect(eng, reads, writes)
        if inc:
            self.cnt[eng] += 1
            tok = [('e', eng), self.cnt[eng], eng]
            for p in self.pending[eng]:
                p[0] = tok[0]
                p[1] = tok[1]
            self.pending[eng] = []
        else:
            tok = [None, None, eng]
            self.pending[eng].append(tok)
        for r in reads:
            r.r.append(tok)
        for w in writes:
            w.w = [tok]
            w.r = []
        self.q[eng].append((fn, waits, inc, None))

    def dma(self, eng, out_ap, in_ap, reads=(), writes=(), out=False, partial=False):
        waits = self._collect(eng, reads, () if partial else writes)
        res = writes[0] if writes else reads[0]
        if eng not in res.dsem:
            res.dsem[eng] = len(self.dres)
            self.dres.append(res)
            res.dcnt[eng] = 0
        res.dcnt[eng] += 1
        tok = [('d', res.dsem[eng]), 16 * res.dcnt[eng], eng]
        for r in reads:
            r.r.append(tok)
        for w in writes:
            if partial:
                w.w.append(tok)
            else:
                w.w = [tok]
                w.r = []
        if out:
            self.final.append(tok)
        self.q[eng].append((('dma_start', (), dict(out=out_ap, in_=in_ap)), waits, False, tok))

    def emit(self):
        nc = self.nc
        for e in ENGS:
            if self.pending[e]:
                raise RuntimeError(f"pending tokens left on {e}")
        with ExitStack() as st:
            esem = {e: st.enter_context(nc.semaphore(f"s_{e}")) for e in ENGS}
            dsem = [st.enter_context(nc.semaphore(f"d_{i}")) for i in range(len(self.dres))]
            block = st.enter_context(nc.Block())

            def handle(key):
                return esem[key[1]] if key[0] == 'e' else dsem[key[1]]

            def run(name, e):
                for fn, waits, inc, dtok in self.q[name]:
                    for key, val in waits:
                        e.wait_ge(handle(key), val)
                    ins = getattr(e, fn[0])(*fn[1], **fn[2])
                    if inc:
                        ins.then_inc(esem[name], 1)
                    if dtok is not None:
                        ins.then_inc(handle(dtok[0]), 16)
                if name == 'sp':
                    fin = {}
                    for key, val, _ in self.final:
                        fin[key] = max(fin.get(key, 0), val)
                    for key, val in fin.items():
                        e.wait_ge(handle(key), val)

            @block.tensor
            def _(e):
                run('pe', e)

            @block.scalar
            def _(e):
                run('act', e)

            @block.vector
            def _(e):
                run('dve', e)

            @block.gpsimd
            def _(e):
                run('pool', e)

            @block.sync
            def _(e):
                run('sp', e)


class Ring:
    def __init__(self, items):
        self.items = items
        self.i = 0

    def get(self):
        it = self.items[self.i % len(self.items)]
        self.i += 1
        return it


def _consts():
    f32 = np.float32
    lg = np.log1p(-np.exp2(-5.0 - np.arange(H, dtype=f32))).astype(f32)
    idx = np.arange(128, dtype=f32)
    c = {}
    c['ident'] = np.eye(128, dtype=f32)
    diff = idx[None, :] - idx[:, None]
    mask = np.zeros((128, H, 128), f32)
    for h in range(H):
        mask[:, h, :] = np.where(diff >= 0, np.exp(np.maximum(diff, 0.0) * lg[h]), 0.0) / 16.0
    c['mask16'] = mask.reshape(128, H * 128)
    c['kdec16'] = (np.exp((127.0 - idx)[:, None] * lg[None, :]) / 16.0).astype(f32)
    cross = np.exp((idx[None, :] + 1.0) * lg[:, None]).astype(f32)
    c['crossb'] = np.ascontiguousarray(np.broadcast_to(cross.reshape(1, H * 128), (128, H * 128)))
    half = 128
    inv = (10000.0 ** (-np.arange(half, dtype=f32) / half)).astype(f32)
    pos = np.arange(T, dtype=f32)
    ang = (inv[:, None] * pos[None, :]).astype(f32)
    c['cosT'] = np.cos(ang).astype(f32)
    c['sinT'] = np.sin(ang).astype(f32)
    angs = (f32(PAST) * inv).astype(f32)
    c['cs_s'] = np.ascontiguousarray(np.broadcast_to(np.cos(angs).astype(f32)[None, :], (NS, 128)))
    c['sn_s'] = np.ascontiguousarray(np.broadcast_to(np.sin(angs).astype(f32)[None, :], (NS, 128)))
    c['causal'] = (diff >= 0).astype(f32)
    dm = np.zeros((128, NS, NS), f32)
    dm[:, np.arange(NS), np.arange(NS)] = 1.0
    c['dmask'] = dm.reshape(128, NS * NS)
    c['dk16'] = (np.eye(NS, dtype=f32) / 16.0)
    gl = [float(np.exp(f32(128.0) * lg[h])) for h in range(H)]
    g1 = [float(np.exp(lg[h])) for h in range(H)]
    return c, gl, g1


CONST_SHAPES = {
    'ident': [128, 128], 'mask16': [128, 512], 'kdec16': [128, 4], 'crossb': [128, 512],
    'cosT': [128, T], 'sinT': [128, T], 'cs_s': [NS, 128], 'sn_s': [NS, 128], 'causal': [128, 128],
    'dmask': [128, NS * NS], 'dk16': [NS, NS],
}
IN_SHAPES = {
    'x': [T, D], 'xs': [NS, D], 'st': [NS, H, 256, 512],
    'w_in_ret': [D, 6144], 'w_out_ret': [2048, D], 'w_in_mlp': [D, 6144], 'w_out_mlp': [2048, D],
    'ln_gain': [2, D], 'ln_bias': [2, D], 'gncol': [128, 16], 'glcol': [128, 16], 'blcol': [128, 16],
    'lgm': [1, 2048], 'lbm': [1, 2048], 'wsT': [128, 1024], 'bsp': [1, 1024], 'ws00': [1, 8], 'bs0': [1, 8],
}
OUT_SHAPES = {'y': [T, D], 'ys': [NS, D], 'sp': [H, 256, 512], 'ss': [NS, H, 256, 512], 'mvs': [NS, 2048]}


def build_program(gl, g1, do_sample=True):
    nc = bass.Bass("TRN2", target_bir_lowering=False)
    di = {}
    for n, s in list(IN_SHAPES.items()) + list(CONST_SHAPES.items()):
        di[n] = nc.dram_tensor(n, s, F32, kind="ExternalInput").ap()
    do = {n: nc.dram_tensor(n, s, F32, kind="ExternalOutput").ap() for n, s in OUT_SHAPES.items()}
    NWIN, NWOUT = 24, 16
    wscr_in = nc.dram_tensor("wscr_in", [NWIN, 128, 4096], BF16).ap()
    wscr_out = nc.dram_tensor("wscr_out", [NWOUT, 128, 2048], BF16).ap()
    fw = FW(nc)
    with ExitStack() as es:
        def sb(name, shape, dt):
            return es.enter_context(nc.sbuf_tensor('sb_' + name, shape, dt))

        def ps(name, shape, dt):
            return es.enter_context(nc.psum_tensor('ps_' + name, shape, dt))

        S32 = sb("S32", [128, 8, 512], F32)
        Sbf = sb("Sbf", [128, 8, 512], BF16)
        S_r = [Res(f"S{i}") for i in range(8)]
        Sb_r = [Res(f"Sb{i}") for i in range(8)]
        identb = sb("identb", [128, 128], BF16)
        cosb = sb("cosb", [128, TB], F32)
        sinb = sb("sinb", [128, TB], F32)
        mask = sb("mask", [128, 4, 128], F32)
        kdec = sb("kdec", [128, 4], F32)
        cross = sb("cross", [128, 4, 128], F32)
        lnb = sb("lnb", [128, 4, 1024], F32)
        gncol = sb("gncol", [128, 16], F32)
        glcol = sb("glcol", [128, 16], F32)
        blcol = sb("blcol", [128, 16], F32)
        wsTb = sb("wsTb", [128, 8, 128], BF16)
        cbT = sb("cbT", [128, 16, 128], F32)
        onesb = sb("onesb", [128, 128], BF16)
        mhalf = sb("mhalf", [128, 1], F32)
        R = sb("R", [128, NCH, 1024], F32)
        XT = sb("XT", [128, 8, TB], BF16)
        xb = [sb(f"xb{i}", [128, 1024], BF16) for i in range(2)]
        slab = [sb(f"slab{i}", [128, 4096], BF16) for i in range(4)]
        NTF = 12
        tmpf = [sb(f"tmpf{i}", [128, 512], F32) for i in range(NTF)]
        tmpk = [sb(f"tmpk{i}", [128, 1024], F32) for i in range(4)]
        stt = [sb(f"stt{i}", [128, 4, 6], F32) for i in range(4)]
        mvt = [sb(f"mvt{i}", [128, 4], F32) for i in range(4)]
        arB = sb("arB", [128, 12288], BF16)
        arF = sb("arF", [128, 6144], F32)
        pf = [ps(f"pf{i}", [128, 512], F32) for i in range(6)]
        pb = [ps(f"pb{i}", [128, 1024], BF16) for i in range(2)]

        C_r = Res("consts")
        cos_r = Res("cos")
        sin_r = Res("sin")
        R_r = [Res(f"R{c}") for c in range(NCH)]
        XT_r = [Res(f"XT{c}") for c in range(NCH)]
        xb_ring = Ring([(xb[i], Res(f"xb{i}")) for i in range(2)])
        win_ring = Ring([(slab[i][:].rearrange("p (k n) -> p k n", k=8), Res(f"slab{i}")) for i in range(3)])
        wout_ring = Ring([(slab[3][:, i * 2048:(i + 1) * 2048].rearrange("p (t n) -> p t n", t=4), Res(f"wo{i}")) for i in range(2)])
        tmpf_it = [(tmpf[i], Res(f"tmpf{i}")) for i in range(NTF)]
        pf_it = [(pf[i], Res(f"pf{i}")) for i in range(6)]
        tmpk_ring = Ring([(tmpk[i], Res(f"tmpk{i}")) for i in range(4)])
        st_ring = Ring([(stt[i], mvt[i], Res(f"st{i}")) for i in range(4)])
        pb_ring = Ring([(pb[i], Res(f"pb{i}")) for i in range(2)])
        tfA, tfB, tfAll = Ring(tmpf_it[0:6]), Ring(tmpf_it[6:12]), Ring(tmpf_it)
        pfA3, pfB3 = Ring(pf_it[0:3]), Ring(pf_it[3:6])
        pfA4, pfB2 = Ring(pf_it[0:4]), Ring(pf_it[4:6])
        pfAll, pf5 = Ring(pf_it), Ring(pf_it[0:5])
        pacc, pacc_r = pf_it[5]

        def v3(ap, a):
            return ap.rearrange("p (a b) -> p a b", a=a)

        wsTf = v3(arF[:, 0:1024], 8)
        bsb = v3(arF[:, 1024:2048], 8)
        causal = arF[:, 2048:2176]
        first = [True]

        def cload(dst, src, eng='sp'):
            fw.dma(eng, dst, src, writes=[C_r], partial=not first[0])
            first[0] = False

        cload(identb[:], di['ident'], 'pool')
        cload(mask[:], v3(di['mask16'], 4))
        cload(kdec[:], di['kdec16'])
        cload(cross[:], v3(di['crossb'], 4))
        for i, (nm, li) in enumerate([('ln_gain', 0), ('ln_bias', 0), ('ln_gain', 1), ('ln_bias', 1)]):
            cload(lnb[:, i, :], di[nm][li, :].partition_broadcast(128))
        cload(gncol[:], di['gncol'])
        cload(glcol[:], di['glcol'])
        cload(blcol[:], di['blcol'])
        H_r = Res("halved")
        fw.op('pool', 'tensor_scalar', glcol[:], glcol[:], 0.5, 1.0, ALU.mult, ALU.mult, reads=[C_r], writes=[H_r])
        M_r = Res("memsets")
        fw.op('pool', 'memset', mhalf[:], -0.5, writes=[M_r], inc=False)
        fw.op('pool', 'memset', onesb[:], 1.0, writes=[M_r])
        setup_r = Res("setup")
        S2_r = Res("setup_in")

        def setup_mlp_tables():
            fw.dma('sp', wsTf, v3(di['wsT'], 8), writes=[S2_r])
            fw.dma('sp', bsb, v3(di['bsp'][0, :].partition_broadcast(128), 8), writes=[S2_r], partial=True)
            fw.dma('sp', causal, di['causal'], writes=[S2_r], partial=True)
            fw.op('dve', 'tensor_tensor', wsTb[:], wsTf, causal.unsqueeze(1).to_broadcast([128, 8, 128]), ALU.mult,
                  reads=[S2_r], writes=[setup_r])
            for half in range(2):
                pt, pr = pfAll.get()
                fw.op('pe', 'matmul', pt[:], onesb[:], wsTb[:, half * 4:(half + 1) * 4, :].rearrange("p a b -> p (a b)"),
                      start=True, stop=True, reads=[setup_r, M_r], writes=[pr])
                for gq in range(4):
                    gi = half * 4 + gq
                    for tt in range(2):
                        ft = gi * 2 + tt
                        fw.op('dve', 'scalar_tensor_tensor', cbT[:, ft, :], pt[:, gq * 128:(gq + 1) * 128], blcol[:, ft:ft + 1],
                              bsb[:, gi, :], ALU.mult, ALU.add, reads=[pr, C_r, S2_r], writes=[setup_r])
            fw.op('dve', 'tensor_scalar', cbT[:], cbT[:], 0.5, None, ALU.mult, reads=[setup_r], writes=[setup_r])

        class WSched:
            def __init__(self, ring, nslots, issue_fn):
                self.plan = []
                self.items = []
                self.issued = 0
                self.released = set()
                self.consumed = 0
                self.ring = ring
                self.ns = nslots
                self.issue_fn = issue_fn

            def issue(self):
                tv, r = self.ring.get()
                self.issue_fn(tv, r, self.plan[self.issued], self.issued)
                self.items.append((tv, r))
                self.issued += 1

            def can_issue(self):
                j = self.issued
                return j < len(self.plan) and (j < self.ns or (j - self.ns) in self.released)

            def pump(self):
                while self.can_issue():
                    self.issue()

            def next(self, *spec):
                assert self.plan[self.consumed][1:] == spec[1:] and self.plan[self.consumed][0] is spec[0], (self.consumed, spec[1:])
                while self.issued <= self.consumed:
                    assert self.can_issue(), ("slab not free", self.consumed)
                    self.issue()
                idx = self.consumed
                self.consumed += 1
                tv, r = self.items[idx]
                return tv, r, idx

            def release(self, *idxs):
                for i in idxs:
                    self.released.add(i)
                self.pump()

        scr_in_r = [Res(f"scr_in{i}") for i in range(NWIN)]
        scr_out_r = [Res(f"scr_out{i}") for i in range(NWOUT)]

        def issue_in(tv, r, spec, j):
            w_ap, pieces = spec
            if j < NWIN:
                for i, (c0, n, off) in enumerate(pieces):
                    fw.dma('pool', tv[:, :, off:off + n], w_ap[:, c0:c0 + n].rearrange("(k p) n -> p k n", p=128),
                           writes=[r], partial=(i > 0))
                fw.dma('sp', wscr_in[j].rearrange("p (k n) -> p k n", k=8), tv, reads=[r], writes=[scr_in_r[j]])
            else:
                jj = j % NWIN
                fw.dma('sp', tv, wscr_in[jj].rearrange("p (k n) -> p k n", k=8), reads=[scr_in_r[jj]], writes=[r])

        def issue_out(tv, r, spec, j):
            w_ap, r0, nh = spec
            if j < NWOUT:
                fw.dma('pool', tv, w_ap[r0:r0 + 512, nh * 512:(nh + 1) * 512].rearrange("(t p) n -> p t n", p=128), writes=[r])
                fw.dma('sp', wscr_out[j].rearrange("p (t n) -> p t n", t=4), tv, reads=[r], writes=[scr_out_r[j]])
            else:
                jj = j % NWOUT
                fw.dma('sp', tv, wscr_out[jj].rearrange("p (t n) -> p t n", t=4), reads=[scr_out_r[jj]], writes=[r])

        Win = WSched(win_ring, 3, issue_in)
        Wout = WSched(wout_ring, 2, issue_out)

        def stats(srcs, np_, src_res):
            stile, mv, r = st_ring.get()
            n = len(srcs)
            for i, s in enumerate(srcs):
                fw.op('dve', 'bn_stats', stile[:np_, i, :], s, reads=src_res, writes=[r], inc=(i == n - 1))
            fw.op('dve', 'bn_aggr', mv[:np_, 0:2], stile[:np_, 0:n, :], reads=[r], writes=[r])
            rstd_chain(mv, r, np_)
            return mv, r

        def rstd_chain(mv, r, np_):
            fw.op('pool', 'tensor_scalar', mv[:np_, 2:3], mv[:np_, 1:2], EPS, None, ALU.add, reads=[r], writes=[r])
            fw.op('pool', 'tensor_tensor', mv[:np_, 2:3], mv[:np_, 2:3], mhalf[:np_, :], ALU.pow, reads=[r, M_r], writes=[r])
            fw.op('pool', 'tensor_scalar', mv[:np_, 3:4], mv[:np_, 0:1], mv[:np_, 2:3], -1.0, ALU.mult, ALU.mult, reads=[r], writes=[r])

        def transposes_to(src, src_res, n, np_, evac):
            pt, pr = pb_ring.get()
            pv = pt[:, 0:n * np_].rearrange("p (a b) -> p a b", a=n)
            for i in range(n):
                fw.op('pe', 'transpose', pv[:, i, :], src[:np_, i * 128:(i + 1) * 128], identb[:np_, :np_],
                      reads=src_res + [C_r], writes=[pr], inc=(i == n - 1))
            evac(pv, pr)

        def layer_norm_chunk(z_ap, z_res, np_, li, out_ap, out_res):
            mv, mr = stats([z_ap[:, 0:512], z_ap[:, 512:1024]], np_, z_res)
            tk, tr = tmpk_ring.get()
            fw.op('act', 'activation', tk[:np_, :], z_ap, AF.Identity, scale=mv[:np_, 2:3], bias=mv[:np_, 3:4],
                  reads=z_res + [mr], writes=[tr])
            fw.op('dve', 'tensor_tensor', tk[:np_, :], tk[:np_, :], lnb[:np_, 2 * li, :], ALU.mult, reads=[tr, C_r], writes=[tr])
            fw.op('dve', 'tensor_tensor', out_ap, tk[:np_, :], lnb[:np_, 2 * li + 1, :], ALU.add, reads=[tr, C_r], writes=out_res)

        ln_active = []

        def gen_ln(z_ap, z_res, li, out_ap, out_res, after):
            stile, mv, r = st_ring.get()
            fw.op('dve', 'bn_stats', stile[:, 0, :], z_ap[:, 0:512], reads=z_res, writes=[r], inc=False)
            fw.op('dve', 'bn_stats', stile[:, 1, :], z_ap[:, 512:1024], reads=z_res, writes=[r])
            fw.op('dve', 'bn_aggr', mv[:, 0:2], stile[:, 0:2, :], reads=[r], writes=[r])
            yield
            rstd_chain(mv, r, 128)
            yield
            tk, tr = tmpk_ring.get()
            fw.op('act', 'activation', tk[:], z_ap, AF.Identity, scale=mv[:, 2:3], bias=mv[:, 3:4], reads=z_res + [r], writes=[tr])
            yield
            fw.op('dve', 'tensor_tensor', tk[:], tk[:], lnb[:, 2 * li, :], ALU.mult, reads=[tr, C_r], writes=[tr])
            fw.op('dve', 'tensor_tensor', out_ap, tk[:], lnb[:, 2 * li + 1, :], ALU.add, reads=[tr, C_r], writes=out_res)
            yield
            after()

        def advance_ln():
            for g in list(ln_active):
                try:
                    next(g)
                except StopIteration:
                    ln_active.remove(g)

        def drain_ln():
            while ln_active:
                advance_ln()

        def accumulate(dst_ap, dst_res, ps_ap, ps_res, first):
            if first:
                fw.op('dve', 'scalar_tensor_tensor', dst_ap, dst_ap, ALPHA, ps_ap, ALU.mult, ALU.add,
                      reads=dst_res + [ps_res], writes=dst_res)
            else:
                fw.op('dve', 'tensor_tensor', dst_ap, dst_ap, ps_ap, ALU.add, reads=dst_res + [ps_res], writes=dst_res)

        def run(g):
            for _ in g:
                pass

        def interleave(a, b):
            alive = [a, b]
            while alive:
                for g in list(alive):
                    try:
                        next(g)
                    except StopIteration:
                        alive.remove(g)

        w_in_ret, w_out_ret, w_in_mlp, w_out_mlp = di['w_in_ret'], di['w_out_ret'], di['w_in_mlp'], di['w_out_mlp']

        def qk_spec(h):
            return (w_in_ret, [(h * 256, 256, 0), (1024 + h * 256, 256, 256)])

        def plan_pass():
            for h in range(H):
                Win.plan.append(qk_spec(h))
                Win.plan.append((w_in_ret, [(2048 + h * 512, 512, 0)]))
                Win.plan.append((w_in_ret, [(4096 + h * 512, 512, 0)]))
                Wout.plan.append((w_out_ret, h * 512, 0))
                Wout.plan.append((w_out_ret, h * 512, 1))
            for s_ in range(4):
                Win.plan.append((w_in_mlp, [(2048 + s_ * 512, 512, 0)]))
            for fg in range(4):
                Win.plan.append((w_in_mlp, [(fg * 512, 512, 0)]))
                Win.plan.append((w_in_mlp, [(4096 + fg * 512, 512, 0)]))
                Wout.plan.append((w_out_mlp, fg * 512, 0))
                Wout.plan.append((w_out_mlp, fg * 512, 1))
        for _ in range(NB + (1 if do_sample else 0)):
            plan_pass()

        def alias_from(dst, src):
            tw = [t for r in src for t in r.w]
            tr = [t for r in src for t in r.r]
            for d in dst:
                d.w = d.w + tw
                d.r = d.r + tr

        qT = [v3(arB[:, (i * 2 + 0) * 1024:(i * 2 + 1) * 1024], 2) for i in range(2)]
        kT = [v3(arB[:, (i * 2 + 1) * 1024:(i * 2 + 2) * 1024], 2) for i in range(2)]
        vh = [v3(arB[:, 4096 + i * 2048:4096 + (i + 1) * 2048], NCH) for i in range(2)]
        gT = [v3(arB[:, 8192 + i * 2048:8192 + (i + 1) * 2048], 4) for i in range(2)]
        sgh = [v3(arF[:, i * 2048:(i + 1) * 2048], NCH) for i in range(2)]
        qT_r = [Res(f"qT{i}") for i in range(2)]
        kT_r = [Res(f"kT{i}") for i in range(2)]
        vh_r = [[Res(f"vh{i}_{c}") for c in range(NCH)] for i in range(2)]
        sg_r = [[Res(f"sg{i}_{c}") for c in range(NCH)] for i in range(2)]
        gT_r = [[Res(f"gT{i}_{c}") for c in range(NCH)] for i in range(2)]
        GV = v3(arB[:, 0:NCH * 2048], NCH)
        GV_r = [Res(f"GV{c}") for c in range(NCH)]
        YT = [v3(arB[:, 8192 + i * 2048:8192 + (i + 1) * 2048], 4) for i in range(2)]
        YT_r = [Res(f"YT{i}") for i in range(2)]
        L0_lo = qT_r + kT_r + [r for l in vh_r for r in l]
        L0_hi = [r for l in gT_r for r in l]
        L0_F = [r for l in sg_r for r in l]

        xb_pref = {}
        ln1_deferred = []

        def prefetch_x(b):
            xb_pref[b] = []
            for c in range(2):
                r0 = b * TB + c * 128
                xt_, xr_ = xb_ring.get()
                fw.dma('pool', xt_[:], di['x'][r0:r0 + 128, :], writes=[xr_])
                xb_pref[b].append((xt_, xr_))

        def gen_loadx(b):
            t0 = b * TB
            fw.dma('sp', cosb[:], di['cosT'][:, t0:t0 + TB], writes=[cos_r])
            fw.dma('sp', sinb[:], di['sinT'][:, t0:t0 + TB], writes=[sin_r])
            for c in range(NCH):
                r0 = t0 + c * 128
                if xb_pref.get(b):
                    xt_, xr_ = xb_pref[b].pop(0)
                else:
                    xt_, xr_ = xb_ring.get()
                    fw.dma('pool', xt_[:], di['x'][r0:r0 + 128, :], writes=[xr_])
                if b == 0 and c == 0:
                    Win.pump()
                    Wout.pump()
                transposes_to(xt_, [xr_], 8, 128,
                              lambda pv, pr, c=c: fw.op('act', 'activation', XT[:, :, c * 128:(c + 1) * 128], pv, AF.Copy,
                                                        reads=[pr], writes=[XT_r[c]]))
                yield

        run(gen_loadx(0))
        for b in range(NB):
            t0 = b * TB
            if b > 0:
                alias_from(L0_lo, GV_r)
                alias_from(L0_hi, YT_r)
            if b == 1:
                alias_from(L0_F, [S2_r])

            def gen_proj(h):
                par = h % 2
                pfP, tfP = (pfAll, tfAll) if h == 0 else (pfB3, tfB)
                sqk, sqk_r, i_qk = Win.next(*qk_spec(h))
                sv, sv_r, i_v = Win.next(w_in_ret, [(2048 + h * 512, 512, 0)])
                sg_, sg_sr, i_g = Win.next(w_in_ret, [(4096 + h * 512, 512, 0)])
                for dstT, dst_r, off in ((qT[par], qT_r[par], 0), (kT[par], kT_r[par], 256)):
                    ts = slice(0, 512)
                    p1, p1r = pfP.get()
                    p2, p2r = pfP.get()
                    for (pt, pr, o2) in ((p1, p1r, off), (p2, p2r, off + 128)):
                        for k in range(8):
                            fw.op('pe', 'matmul', pt[:], sqk[:, k, o2:o2 + 128], XT[:, k, ts], start=(k == 0), stop=(k == 7),
                                  reads=[sqk_r] + XT_r, writes=[pr], inc=(k == 7))
                        yield
                    ta, tar = tfP.get()
                    tb_, tbr = tfP.get()
                    tc, tcr = tfP.get()
                    td, tdr = tfP.get()
                    fw.op('dve', 'tensor_tensor', ta[:], p1[:], cosb[:, ts], ALU.mult, reads=[p1r, cos_r], writes=[tar])
                    fw.op('dve', 'tensor_tensor', tb_[:], p2[:], sinb[:, ts], ALU.mult, reads=[p2r, sin_r], writes=[tbr])
                    fw.op('dve', 'tensor_tensor', tc[:], p2[:], cosb[:, ts], ALU.mult, reads=[p2r, cos_r], writes=[tcr])
                    fw.op('dve', 'tensor_tensor', td[:], p1[:], sinb[:, ts], ALU.mult, reads=[p1r, sin_r], writes=[tdr])
                    fw.op('pool', 'tensor_tensor', dstT[:, 0, ts], ta[:], tb_[:], ALU.subtract, reads=[tar, tbr], writes=[dst_r])
                    fw.op('pool', 'tensor_tensor', dstT[:, 1, ts], tc[:], td[:], ALU.add, reads=[tcr, tdr, dst_r], writes=[dst_r])
                Win.release(i_qk)
                for c in range(NCH):
                    cs = slice(c * 128, (c + 1) * 128)
                    pv_, pvr = pfP.get()
                    for k in range(8):
                        fw.op('pe', 'matmul', pv_[:], XT[:, k, cs], sv[:, k, :], start=(k == 0), stop=(k == 7),
                              reads=[sv_r, XT_r[c]], writes=[pvr], inc=(k == 7))
                    fw.op('act', 'activation', vh[par][:, c, :], pv_[:], AF.Copy, reads=[pvr], writes=[vh_r[par][c]])
                    yield
                    pg_, pgr = pfP.get()
                    for k in range(8):
                        fw.op('pe', 'matmul', pg_[:], XT[:, k, cs], sg_[:, k, :], start=(k == 0), stop=(k == 7),
                              reads=[sg_sr, XT_r[c]], writes=[pgr], inc=(k == 7))
                    fw.op('act', 'activation', sgh[par][:, c, :], pg_[:], AF.Silu, reads=[pgr], writes=[sg_r[par][c]])
                    yield
                Win.release(i_v, i_g)

            xt_pending = {}
            ln0_deferred = []

            def flush_xt(c):
                if c in xt_pending:
                    xt_, xr_ = xt_pending.pop(c)
                    transposes_to(xt_, [xr_], 8, 128,
                                  lambda pv, pr: fw.op('dve', 'tensor_copy', XT[:, :, c * 128:(c + 1) * 128], pv,
                                                       reads=[pr], writes=[XT_r[c]]))

            def gen_ret(h):
                par = h % 2
                wo = [Wout.next(w_out_ret, h * 512, nh) for nh in range(2)]

                def tail(c, gt, gtr):
                    cs = slice(c * 128, (c + 1) * 128)
                    transposes_to(gt, [gtr], 4, 128,
                                  lambda pv, pr: fw.op('dve', 'tensor_tensor', gT[par][:, :, cs], pv,
                                                       gncol[:, h * 4:(h + 1) * 4].unsqueeze(2).to_broadcast([128, 4, 128]), ALU.mult,
                                                       reads=[pr, C_r], writes=[gT_r[par][c]]))
                    yield
                    for nh in range(2):
                        py, pyr = pfA3.get()
                        for t in range(4):
                            fw.op('pe', 'matmul', py[:], gT[par][:, t, cs], wo[nh][0][:, t, :], start=(t == 0), stop=(t == 3),
                                  reads=[gT_r[par][c], wo[nh][1]], writes=[pyr], inc=(t == 3))
                        accumulate(R[:, c, nh * 512:(nh + 1) * 512], [R_r[c]], py[:], pyr, h == 0)
                        yield
                    if h == H - 1:
                        if c >= 1:
                            flush_xt(c - 1)

                        def after(c=c):
                            for c2 in sorted(xt_pending):
                                if c2 <= c - 2:
                                    flush_xt(c2)
                            xt_, xr_ = xb_ring.get()
                            fw.op('act', 'activation', xt_[:], R[:, c, :], AF.Copy, reads=[R_r[c]], writes=[xr_])
                            xt_pending[c] = (xt_, xr_)
                        ln0_deferred.append(gen_ln(R[:, c, :], [R_r[c]], 0, R[:, c, :], [R_r[c]], after))

                pend = None
                for c in range(NCH):
                    gc = b * NCH + c
                    cs = slice(c * 128, (c + 1) * 128)
                    psc, pscr = pfA3.get()
                    for dt in range(2):
                        fw.op('pe', 'matmul', psc[:, 0:128], kT[par][:, dt, cs], qT[par][:, dt, cs], start=(dt == 0), stop=(dt == 1),
                              reads=[kT_r[par], qT_r[par]], writes=[pscr], inc=(dt == 1))
                    sct, sctr = tfA.get()
                    scb = sct[:].bitcast(BF16)[:, 0:128]
                    fw.op('dve', 'tensor_tensor', scb, psc[:, 0:128], mask[:, h, :], ALU.mult, reads=[pscr, C_r], writes=[sctr])
                    kdt, kdr = tfA.get()
                    kd = kdt[:].bitcast(BF16)[:, 0:256]
                    pkt, pkr = pb_ring.get()
                    for dt in range(2):
                        fw.op('pe', 'transpose', pkt[:, dt * 128:(dt + 1) * 128], kT[par][:, dt, cs], identb[:],
                              reads=[kT_r[par], C_r], writes=[pkr], inc=(dt == 1))
                    fw.op('act', 'activation', kd, pkt[:, 0:256], AF.Copy, scale=kdec[:, h:h + 1], reads=[pkr, C_r], writes=[kdr])
                    if gc > 0:
                        qst, qsr = tfA.get()
                        qs = qst[:].bitcast(BF16)[:, 0:256].rearrange("p (a b) -> p a b", a=2)
                        fw.op('pool', 'tensor_tensor', qs, qT[par][:, :, cs], cross[:, h, :].unsqueeze(1).to_broadcast([128, 2, 128]), ALU.mult,
                              reads=[qT_r[par], C_r], writes=[qsr])
                    yield
                    po, por = pfA3.get()
                    fw.op('pe', 'matmul', po[:], scb, vh[par][:, c, :], start=True, stop=(gc == 0),
                          reads=[sctr, vh_r[par][c]], writes=[por], inc=(gc == 0))
                    if gc > 0:
                        for dt in range(2):
                            fw.op('pe', 'matmul', po[:], qs[:, dt, :], Sbf[:, h * 2 + dt, :], start=False, stop=(dt == 1),
                                  reads=[qsr, Sb_r[h * 2 + dt]], writes=[por], inc=(dt == 1))
                    mv, mr = stats([po[:]], 128, [por])
                    on, onr = tfA.get()
                    fw.op('act', 'activation', on[:], po[:], AF.Identity, scale=mv[:, 2:3], bias=mv[:, 3:4], reads=[por, mr], writes=[onr])
                    gtt, gtr = tfA.get()
                    gt = gtt[:].bitcast(BF16)[:, 0:512]
                    fw.op('pool', 'tensor_tensor', gt, on[:], sgh[par][:, c, :], ALU.mult, reads=[onr, sg_r[par][c]], writes=[gtr])
                    yield
                    for dt in range(2):
                        si = h * 2 + dt
                        pu, pur = pfA3.get()
                        fw.op('pe', 'matmul', pu[:], kd[:, dt * 128:(dt + 1) * 128], vh[par][:, c, :], start=True, stop=True,
                              reads=[kdr, vh_r[par][c]], writes=[pur])
                        if gc == 0:
                            fw.op('dve', 'tensor_copy', S32[:, si, :], pu[:], reads=[pur], writes=[S_r[si]])
                        else:
                            fw.op('dve', 'scalar_tensor_tensor', S32[:, si, :], S32[:, si, :], gl[h], pu[:], ALU.mult, ALU.add,
                                  reads=[pur, S_r[si]], writes=[S_r[si]])
                        if gc < NB * NCH - 1:
                            fw.op('act', 'activation', Sbf[:, si, :], S32[:, si, :], AF.Copy, reads=[S_r[si]], writes=[Sb_r[si]])
                    yield
                    if pend is not None:
                        yield from tail(*pend)
                    pend = (c, gt, gtr)
                yield from tail(*pend)
                Wout.release(wo[0][2], wo[1][2])

            for _ in gen_proj(0):
                if ln1_deferred:
                    ln_active.append(ln1_deferred.pop(0))
                advance_ln()
            while ln1_deferred or ln_active:
                if ln1_deferred:
                    ln_active.append(ln1_deferred.pop(0))
                advance_ln()
            for c in range(NCH):
                r0 = t0 + c * 128
                fw.dma('sp', R[:, c, :], di['x'][r0:r0 + 128, :], writes=[R_r[c]])
            for h in range(H):
                if h < H - 1:
                    interleave(gen_ret(h), gen_proj(h + 1))
                else:
                    run(gen_ret(h))
                    while ln0_deferred or ln_active:
                        if ln0_deferred:
                            ln_active.append(ln0_deferred.pop(0))
                        advance_ln()

            if b == NB - 1:
                fw.dma('sp', do['sp'].rearrange("h (dt p) e -> p (h dt) e", p=128), S32[:], reads=S_r, out=True)

            alias_from(GV_r, L0_lo)
            alias_from(YT_r, L0_hi)
            if b == 0:
                alias_from([S2_r], L0_F)
                setup_mlp_tables()
            vst = [st_ring.get() for _ in range(NCH)]
            for s in range(4):
                sl, slr, i_sl = Win.next(w_in_mlp, [(2048 + s * 512, 512, 0)])
                for c in range(NCH):
                    flush_xt(c)
                    cs = slice(c * 128, (c + 1) * 128)
                    pv_, pvr = pfAll.get()
                    for k in range(8):
                        fw.op('pe', 'matmul', pv_[:], XT[:, k, cs], sl[:, k, :], start=(k == 0), stop=(k == 7),
                              reads=[slr, XT_r[c]], writes=[pvr], inc=(k == 7))
                    tg_, tgr = tfAll.get()
                    fw.op('act', 'activation', tg_[:], pv_[:], AF.Gelu, reads=[pvr], writes=[tgr])
                    stile, mv, str_ = vst[c]
                    fw.op('dve', 'bn_stats', stile[:, s, :], tg_[:], reads=[tgr], writes=[str_])
                    fw.op('dve', 'tensor_copy', GV[:, c, s * 512:(s + 1) * 512], tg_[:], reads=[tgr], writes=[GV_r[c]])
                Win.release(i_sl)
            for c in range(NCH):
                stile, mv, r = vst[c]
                fw.op('dve', 'bn_aggr', mv[:, 0:2], stile[:, 0:4, :], reads=[r], writes=[r])
                rstd_chain(mv, r, 128)
                fw.op('act', 'activation', GV[:, c, :], GV[:, c, :], AF.Identity, scale=mv[:, 2:3], bias=mv[:, 3:4],
                      reads=[r, GV_r[c]], writes=[GV_r[c]])

            if b < NB - 1:
                prefetch_x(b + 1)

            def gen_ug(fg):
                par = fg % 2
                su, sur, i_u = Win.next(w_in_mlp, [(fg * 512, 512, 0)])
                sgl, sglr, i_g = Win.next(w_in_mlp, [(4096 + fg * 512, 512, 0)])
                ts = slice(0, 512)
                for t in range(4):
                    ft = fg * 4 + t
                    gi = ft // 2
                    pu, pur = pfA4.get()
                    pg_, pgr = pfA4.get()
                    pm, pmr = pfA4.get()
                    for (pt, pr, sl_, slr_) in ((pu, pur, su, sur), (pg_, pgr, sgl, sglr)):
                        for k in range(8):
                            fw.op('pe', 'matmul', pt[:], sl_[:, k, t * 128:(t + 1) * 128], XT[:, k, ts], start=(k == 0), stop=(k == 7),
                                  reads=[slr_] + XT_r, writes=[pr], inc=(k == 7))
                        yield
                    for cc in range(4):
                        fw.op('pe', 'matmul', pm[:, cc * 128:(cc + 1) * 128], GV[:, cc, ft * 128:(ft + 1) * 128], wsTb[:, gi, :],
                              start=True, stop=True, reads=[GV_r[cc], setup_r], writes=[pmr], inc=(cc == 3))
                    gu, gur = tfAll.get()
                    sgt, sgtr = tfAll.get()
                    mm, mmr = tfAll.get()
                    fw.op('act', 'activation', gu[:], pu[:], AF.Gelu, reads=[pur], writes=[gur])
                    fw.op('act', 'activation', sgt[:], pg_[:], AF.Tanh, scale=0.5, reads=[pgr], writes=[sgtr])
                    fw.op('dve', 'scalar_tensor_tensor', sgt[:], sgt[:], 1.0, pg_[:], ALU.add, ALU.mult, reads=[sgtr, pgr], writes=[sgtr])
                    fw.op('dve', 'scalar_tensor_tensor', v3(mm[:], 4), v3(pm[:], 4), glcol[:, ft:ft + 1],
                          cbT[:, ft, :].unsqueeze(1).to_broadcast([128, 4, 128]), ALU.mult, ALU.add,
                          reads=[pmr, C_r, H_r, setup_r], writes=[mmr])
                    fw.op('dve', 'tensor_tensor', gu[:], gu[:], sgt[:], ALU.mult, reads=[gur, sgtr], writes=[gur])
                    fw.op('pool', 'tensor_tensor', YT[par][:, t, ts], gu[:], mm[:], ALU.mult, reads=[gur, mmr], writes=[YT_r[par]])
                    yield
                Win.release(i_u, i_g)

            def gen_out(fg):
                par = fg % 2
                for nh in range(2):
                    wo, wo_r, i_wo = Wout.next(w_out_mlp, fg * 512, nh)
                    for c in range(NCH):
                        cs = slice(c * 128, (c + 1) * 128)
                        py, pyr = pfB2.get()
                        for t in range(4):
                            fw.op('pe', 'matmul', py[:], YT[par][:, t, cs], wo[:, t, :], start=(t == 0), stop=(t == 3),
                                  reads=[YT_r[par], wo_r], writes=[pyr], inc=(t == 3))
                        accumulate(R[:, c, nh * 512:(nh + 1) * 512], [R_r[c]], py[:], pyr, fg == 0)
                        if fg == 3 and nh == 1:
                            def after(c=c, t0=t0):
                                r0 = t0 + c * 128
                                fw.dma('sp', do['y'][r0:r0 + 128, :], R[:, c, :], reads=[R_r[c]], out=True)
                            ln1_deferred.append(gen_ln(R[:, c, :], [R_r[c]], 1, R[:, c, :], [R_r[c]], after))
                        yield
                    Wout.release(i_wo)

            run(gen_ug(0))
            for fg in range(4):
                g_out = gen_out(fg)
                if fg < 3:
                    g_n = gen_ug(fg + 1)
                    next(g_n)
                    interleave(g_out, g_n)
                elif b < NB - 1:
                    interleave(g_out, gen_loadx(b + 1))
                else:
                    run(g_out)
                    while ln1_deferred or ln_active:
                        if ln1_deferred:
                            ln_active.append(ln1_deferred.pop(0))
                        advance_ln()

        if do_sample:
            units = [(h, bb) for h in range(H) for bb in range(NS)]
            loaded = []

            def load_unit(u):
                h_, bb_ = units[u]
                stt_, str_ = tmpk_ring.get()
                sv3_ = stt_[:].rearrange("p (a b) -> p a b", a=2)
                fw.dma('sp', sv3_, di['st'][bb_, h_].rearrange("(dt p) e -> p dt e", p=128), writes=[str_])
                loaded.append((sv3_, str_))
            for u in range(3):
                load_unit(u)
            fw.barrier()
            Rs = arF[0:NS, 0:1024]
            gvs = arF[0:NS, 1024:3072]
            lgb = arF[0:NS, 3072:4096]
            lbb = arF[0:NS, 4096:5120]
            csn = arF[0:NS, 5120:5376]
            dk16 = arF[0:NS, 5376:5392]
            wb8 = arF[0:NS, 5392:5408]
            o = [0]

            def ab(n, np_=128):
                a = arB[0:np_, o[0]:o[0] + n]
                o[0] += n
                return a
            xsb = ab(1024, NS)
            XTs = v3(ab(8 * NS), 8)
            qkb = ab(512, NS)
            vsb = ab(512, NS)
            kmb = [ab(256, NS) for _ in range(2)]
            qTs = v3(ab(2 * NS), 2)
            qTm = ab(2 * NS * NS).rearrange("p (a b c) -> p a b c", a=2, b=NS)
            s1b = [v3(ab(1024), 2) for _ in range(2)]
            gts = ab(2048, NS)
            gTs = v3(ab(16 * NS), 16)
            ysb = ab(2048, NS)
            yTs = v3(ab(16 * NS), 16)
            dmk = ab(NS * NS).rearrange("p (b c) -> p b c", b=NS)
            A_r = Res("sconst")
            Rs_r = Res("Rs")
            XTs_r = Res("XTs")
            fw.dma('sp', Rs, di['xs'], writes=[Rs_r])
            xsb_r = Res("xsb")
            fw.dma('pool', xsb, di['xs'], writes=[xsb_r])
            fw.dma('sp', csn[:, 0:128], di['cs_s'], writes=[A_r])
            fw.dma('sp', csn[:, 128:256], di['sn_s'], writes=[A_r], partial=True)
            fw.dma('sp', dk16, di['dk16'], writes=[A_r], partial=True)
            fw.dma('sp', wb8[:, 0:8], di['ws00'][0, :].partition_broadcast(NS), writes=[A_r], partial=True)
            fw.dma('sp', wb8[:, 8:16], di['bs0'][0, :].partition_broadcast(NS), writes=[A_r], partial=True)
            fw.dma('pool', dmk, v3(di['dmask'], NS), writes=[A_r], partial=True)
            transposes_to(xsb, [xsb_r], 8, NS,
                          lambda pv, pr: fw.op('dve', 'tensor_copy', XTs, pv, reads=[pr], writes=[XTs_r]))
            cs2 = csn[:, 0:128].unsqueeze(1).to_broadcast([NS, 2, 128])
            sn2 = csn[:, 128:256].unsqueeze(1).to_broadcast([NS, 2, 128])
            gts_r = Res("gts")
            qkb_r, vs_r, qTs_r, qTm_r, gTs_r = Res("qkb"), Res("vs"), Res("qTs"), Res("qTm"), Res("gTs")
            s1b_ring = Ring([(s1b[i], Res(f"s1b{i}")) for i in range(2)])
            km_ring = Ring([(kmb[i], Res(f"km{i}")) for i in range(2)])


            def proj(sl, slr):
                pt, pr = pf5.get()
                for k in range(8):
                    fw.op('pe', 'matmul', pt[0:NS, :], XTs[:, k, :], sl[:, k, :], start=(k == 0), stop=(k == 7),
                          reads=[slr, XTs_r], writes=[pr], inc=(k == 7))
                return pt, pr


            qkb_s = [qkb, ab(512, NS)]
            vsb_s = [vsb, ab(512, NS)]
            qTs_s = [qTs, v3(ab(2 * NS), 2)]
            qTm_s = [qTm, ab(2 * NS * NS).rearrange("p (a b c) -> p a b c", a=2, b=NS)]
            qkb_rs = [qkb_r, Res("qkb1")]
            vs_rs = [vs_r, Res("vs1")]
            qTs_rs = [qTs_r, Res("qTs1")]
            qTm_rs = [qTm_r, Res("qTm1")]
            head = {}

            def pre(h):
                st_ = h % 2
                qkb_, qkbr_, vsb_, vsr_ = qkb_s[st_], qkb_rs[st_], vsb_s[st_], vs_rs[st_]
                sqk, sqk_r, i_qk = Win.next(*qk_spec(h))
                sv, sv_r, i_v = Win.next(w_in_ret, [(2048 + h * 512, 512, 0)])
                sg_, sg_sr, i_g = Win.next(w_in_ret, [(4096 + h * 512, 512, 0)])
                pq, pqr = proj(sqk, sqk_r)
                Win.release(i_qk)
                qk, qkr = tfAll.get()
                fw.op('act', 'activation', qk[0:NS, :], pq[0:NS, :], AF.Copy, reads=[pqr], writes=[qkr])
                qk4 = qk[0:NS, :].rearrange("p (a b c) -> p a b c", a=2, b=2)
                x1, x2 = qk4[:, :, 0, :], qk4[:, :, 1, :]
                tr_, trr = tfAll.get()
                t4 = tr_[0:NS, :].rearrange("p (a b) -> p a b", a=4)
                qkb4 = qkb_.rearrange("p (a b c) -> p a b c", a=2, b=2)
                fw.op('dve', 'tensor_tensor', t4[:, 0:2, :], x1, cs2, ALU.mult, reads=[qkr, A_r], writes=[trr])
                fw.op('dve', 'tensor_tensor', t4[:, 2:4, :], x2, sn2, ALU.mult, reads=[qkr, A_r, trr], writes=[trr])
                fw.op('dve', 'tensor_tensor', qkb4[:, :, 0, :], t4[:, 0:2, :], t4[:, 2:4, :], ALU.subtract, reads=[trr], writes=[qkbr_])
                fw.op('dve', 'tensor_tensor', t4[:, 0:2, :], x2, cs2, ALU.mult, reads=[qkr, A_r, qkbr_], writes=[trr])
                fw.op('dve', 'tensor_tensor', t4[:, 2:4, :], x1, sn2, ALU.mult, reads=[qkr, A_r, trr], writes=[trr])
                fw.op('dve', 'tensor_tensor', qkb4[:, :, 1, :], t4[:, 0:2, :], t4[:, 2:4, :], ALU.add, reads=[trr, qkbr_], writes=[qkbr_])
                pv_, pvr = proj(sv, sv_r)
                Win.release(i_v)
                fw.op('act', 'activation', vsb_, pv_[0:NS, :], AF.Copy, reads=[pvr], writes=[vsr_])
                pg_, pgr = proj(sg_, sg_sr)
                Win.release(i_g)
                sgs, sgsr = tfAll.get()
                fw.op('act', 'activation', sgs[0:NS, :], pg_[0:NS, :], AF.Silu, reads=[pgr], writes=[sgsr])
                transposes_to(qkb_, [qkbr_], 2, NS,
                              lambda pv, pr: fw.op('dve', 'tensor_copy', qTs_s[st_], pv, reads=[pr], writes=[qTs_rs[st_]]))
                fw.op('dve', 'tensor_tensor', qTm_s[st_], qTs_s[st_].unsqueeze(2).to_broadcast([128, 2, NS, NS]),
                      dmk.unsqueeze(1).to_broadcast([128, 2, NS, NS]), ALU.mult, reads=[qTs_rs[st_], A_r], writes=[qTm_rs[st_]])
                head[h] = (sgs, sgsr)

            def post_pe(h):
                woh = [Wout.next(w_out_ret, h * 512, nh) for nh in range(2)]
                transposes_to(gts[:, h * 512:(h + 1) * 512], [gts_r], 4, NS,
                              lambda pv, pr: fw.op('dve', 'tensor_tensor', gTs[:, h * 4:(h + 1) * 4, :], pv,
                                                   gncol[:, h * 4:(h + 1) * 4].unsqueeze(2).to_broadcast([128, 4, NS]), ALU.mult,
                                                   reads=[pr, C_r], writes=[gTs_r]))
                for nh in range(2):
                    py, pyr = pf5.get()
                    for t in range(4):
                        fw.op('pe', 'matmul', py[0:NS, :], gTs[:, h * 4 + t, :], woh[nh][0][:, t, :], start=(t == 0), stop=(t == 3),
                              reads=[gTs_r, woh[nh][1]], writes=[pyr], inc=(t == 3))
                    accumulate(Rs[:, nh * 512:(nh + 1) * 512], [Rs_r], py[0:NS, :], pyr, h == 0)
                Wout.release(woh[0][2], woh[1][2])

            pre(0)
            for h in range(H):
                st_ = h % 2
                qkb_, qkbr_, vsb_, vsr_ = qkb_s[st_], qkb_rs[st_], vsb_s[st_], vs_rs[st_]
                qTm_, qTmr_ = qTm_s[st_], qTm_rs[st_]
                sgs, sgsr = head[h]
                if h + 1 < H:
                    pre(h + 1)
                pos_, posr = pacc, pacc_r
                prev = None
                for bb in range(NS):
                    u = h * NS + bb
                    sv3, str_ = loaded[u]
                    km, kmr = km_ring.get()
                    fw.op('pool', 'tensor_scalar', km, qkb_[:, 256:512], dk16[:, bb:bb + 1], 1.0, ALU.mult, ALU.mult,
                          reads=[qkbr_, A_r], writes=[kmr])
                    s1, s1r = s1b_ring.get()
                    for dt in range(2):
                        pu, pur = pf5.get()
                        fw.op('pe', 'matmul', pu[:], km[:, dt * 128:(dt + 1) * 128], vsb_, start=True, stop=True,
                              reads=[kmr, vsr_], writes=[pur])
                        fw.op('dve', 'scalar_tensor_tensor', sv3[:, dt, :], sv3[:, dt, :], g1[h], pu[:], ALU.mult, ALU.add,
                              reads=[pur, str_], writes=[str_])
                    fw.op('act', 'activation', s1, sv3, AF.Copy, reads=[str_], writes=[s1r])
                    fw.dma('act', do['ss'][bb, h].rearrange("(dt p) e -> p dt e", p=128), sv3, reads=[str_], out=True)
                    if u + 3 < len(units):
                        load_unit(u + 3)
                    for (pbb, ps1, ps1r) in ([prev] if prev is not None else []) + ([(bb, s1, s1r)] if bb == NS - 1 else []):
                        for dt in range(2):
                            fw.op('pe', 'matmul', pos_[0:NS, :], qTm_[:, dt, pbb, :], ps1[:, dt, :],
                                  start=(pbb == 0 and dt == 0), stop=(pbb == NS - 1 and dt == 1),
                                  reads=[qTmr_, ps1r], writes=[posr], inc=True)
                    prev = (bb, s1, s1r)
                    if bb == 5 and h > 0:
                        post_pe(h - 1)
                mv, mr = stats([pos_[0:NS, :]], NS, [posr])
                on, onr = tfAll.get()
                fw.op('act', 'activation', on[0:NS, :], pos_[0:NS, :], AF.Identity, scale=mv[0:NS, 2:3], bias=mv[0:NS, 3:4],
                      reads=[posr, mr], writes=[onr])
                fw.op('pool', 'tensor_tensor', gts[:, h * 512:(h + 1) * 512], on[0:NS, :], sgs[0:NS, :], ALU.mult,
                      reads=[onr, sgsr], writes=[gts_r])
            post_pe(H - 1)
            layer_norm_chunk(Rs, [Rs_r], NS, 0, Rs, [Rs_r])
            fw.op('act', 'activation', xsb, Rs, AF.Copy, reads=[Rs_r], writes=[xsb_r])
            transposes_to(xsb, [xsb_r], 8, NS,
                          lambda pv, pr: fw.op('dve', 'tensor_copy', XTs, pv, reads=[pr], writes=[XTs_r]))
            gvs_r = Res("gvs")
            for s in range(4):
                sl, slr, i_sl = Win.next(w_in_mlp, [(2048 + s * 512, 512, 0)])
                pt, pr = proj(sl, slr)
                Win.release(i_sl)
                fw.op('act', 'activation', gvs[:, s * 512:(s + 1) * 512], pt[0:NS, :], AF.Gelu, reads=[pr, gvs_r], writes=[gvs_r])
            mv, mr = stats([gvs[:, s * 512:(s + 1) * 512] for s in range(4)], NS, [gvs_r])
            fw.op('act', 'activation', gvs, gvs, AF.Identity, scale=mv[0:NS, 2:3], bias=mv[0:NS, 3:4], reads=[gvs_r, mr], writes=[gvs_r])
            lg_r = Res("lgb")
            for hf in range(2):
                fw.dma('sp', lgb, di['lgm'][0, hf * 1024:(hf + 1) * 1024].partition_broadcast(NS), writes=[lg_r])
                fw.dma('sp', lbb, di['lbm'][0, hf * 1024:(hf + 1) * 1024].partition_broadcast(NS), writes=[lg_r], partial=True)
                fw.op('pool', 'tensor_tensor', gvs[:, hf * 1024:(hf + 1) * 1024], gvs[:, hf * 1024:(hf + 1) * 1024], lgb, ALU.mult,
                      reads=[gvs_r, lg_r], writes=[gvs_r])
                fw.op('pool', 'tensor_tensor', gvs[:, hf * 1024:(hf + 1) * 1024], gvs[:, hf * 1024:(hf + 1) * 1024], lbb, ALU.add,
                      reads=[gvs_r, lg_r], writes=[gvs_r])
            fw.dma('sp', do['mvs'], gvs, reads=[gvs_r], out=True)
            mxt, mxr = tmpk_ring.get()
            mxt2, mxr2 = tmpk_ring.get()
            mx3a = mxt[0:NS, :].rearrange("p (g d) -> p g d", g=4)
            mx3b = mxt2[0:NS, :].rearrange("p (g d) -> p g d", g=4)
            gv3 = gvs.rearrange("p (g d) -> p g d", g=8)
            for hf, (m3, mr_) in enumerate(((mx3a, mxr), (mx3b, mxr2))):
                fw.op('dve', 'tensor_tensor', m3, gv3[:, hf * 4:(hf + 1) * 4, :],
                      wb8[:, hf * 4:(hf + 1) * 4].unsqueeze(2).to_broadcast([NS, 4, 256]), ALU.mult, reads=[gvs_r, A_r], writes=[mr_])
                fw.op('dve', 'tensor_tensor', m3, m3, wb8[:, 8 + hf * 4:8 + (hf + 1) * 4].unsqueeze(2).to_broadcast([NS, 4, 256]), ALU.add,
                      reads=[mr_, A_r], writes=[mr_])
            ys_r = Res("ysb")
            yTs_r = Res("yTs")
            for fg in range(4):
                su, sur, i_u = Win.next(w_in_mlp, [(fg * 512, 512, 0)])
                sgl, sglr, i_g = Win.next(w_in_mlp, [(4096 + fg * 512, 512, 0)])
                woh = [Wout.next(w_out_mlp, fg * 512, nh) for nh in range(2)]
                pu, pur = proj(su, sur)
                pg_, pgr = proj(sgl, sglr)
                Win.release(i_u, i_g)
                gu, gur = tfAll.get()
                sgt, sgtr = tfAll.get()
                fw.op('act', 'activation', gu[0:NS, :], pu[0:NS, :], AF.Gelu, reads=[pur], writes=[gur])
                fw.op('act', 'activation', sgt[0:NS, :], pg_[0:NS, :], AF.Silu, reads=[pgr], writes=[sgtr])
                mxs = (mxt if fg < 2 else mxt2)[0:NS, (fg % 2) * 512:(fg % 2 + 1) * 512]
                mxs_r = mxr if fg < 2 else mxr2
                fw.op('pool', 'tensor_tensor', gu[0:NS, :], gu[0:NS, :], sgt[0:NS, :], ALU.mult, reads=[gur, sgtr], writes=[gur])
                fw.op('pool', 'tensor_tensor', ysb[:, fg * 512:(fg + 1) * 512], gu[0:NS, :], mxs, ALU.mult, reads=[gur, mxs_r], writes=[ys_r])
                transposes_to(ysb[:, fg * 512:(fg + 1) * 512], [ys_r], 4, NS,
                              lambda pv, pr: fw.op('dve', 'tensor_copy', yTs[:, fg * 4:(fg + 1) * 4, :], pv, reads=[pr], writes=[yTs_r]))
                for nh in range(2):
                    py, pyr = pf5.get()
                    for t in range(4):
                        fw.op('pe', 'matmul', py[0:NS, :], yTs[:, fg * 4 + t, :], woh[nh][0][:, t, :], start=(t == 0), stop=(t == 3),
                              reads=[yTs_r, woh[nh][1]], writes=[pyr], inc=(t == 3))
                    accumulate(Rs[:, nh * 512:(nh + 1) * 512], [Rs_r], py[0:NS, :], pyr, fg == 0)
                Wout.release(woh[0][2], woh[1][2])
            ot, otr = tmpk_ring.get()
            layer_norm_chunk(Rs, [Rs_r], NS, 1, ot[0:NS, :], [otr])
            fw.dma('sp', do['ys'], ot[0:NS, :], reads=[otr], out=True)

        fw.emit()
    return nc


_CACHE = {}


def kernel(x_prompt, x_sample, state_ret, ln_gain, ln_bias, w_in_ret, gn_gain_ret, w_out_ret,
           w_in_mlp, ln_gain_mlp, ln_bias_mlp, w_spatial, b_spatial, w_out_mlp):
    f32 = np.float32
    A = lambda a: np.ascontiguousarray(np.asarray(a, dtype=f32))
    consts, gl, g1 = _consts()
    if 'nc' not in _CACHE:
        _CACHE['nc'] = build_program(gl, g1)
    nc = _CACHE['nc']
    x_prompt = A(x_prompt)
    x_sample = A(x_sample)
    state_ret = A(state_ret)
    shared = {
        'w_in_ret': A(w_in_ret)[0], 'w_out_ret': A(w_out_ret)[0], 'w_in_mlp': A(w_in_mlp)[0], 'w_out_mlp': A(w_out_mlp)[0],
        'ln_gain': A(ln_gain), 'ln_bias': A(ln_bias),
        'gncol': A(A(gn_gain_ret)[0].reshape(16, 128).T), 'glcol': A(A(ln_gain_mlp)[0].reshape(16, 128).T),
        'blcol': A(A(ln_bias_mlp)[0].reshape(16, 128).T),
        'lgm': A(ln_gain_mlp).reshape(1, 2048), 'lbm': A(ln_bias_mlp).reshape(1, 2048),
        'wsT': A(A(w_spatial)[0].transpose(2, 0, 1).reshape(128, 1024)),
        'bsp': A(b_spatial).reshape(1, 1024),
        'ws00': A(A(w_spatial)[0][:, 0, 0].reshape(1, 8)), 'bs0': A(A(b_spatial)[0][:, 0].reshape(1, 8)),
    }
    shared.update(consts)
    in_maps = []
    for c in range(N_CORES):
        m = dict(shared)
        m['x'] = x_prompt[c]
        m['xs'] = A(x_sample[c * NS:(c + 1) * NS, 0, :])
        m['st'] = A(state_ret[0, c * NS:(c + 1) * NS])
        in_maps.append(m)
    res = run_bass_kernel_spmd(nc, in_maps, core_ids=list(range(N_CORES)))
    rs = res.results
    y_prompt = np.stack([rs[c]['y'] for c in range(N_CORES)], 0).astype(f32)
    y_sample = np.concatenate([rs[c]['ys'] for c in range(N_CORES)], 0).reshape(128, 1, D).astype(f32)
    ret_p = np.stack([rs[c]['sp'] for c in range(N_CORES)], 0)[None].astype(f32)
    ret_s = np.concatenate([rs[c]['ss'] for c in range(N_CORES)], 0)[None].astype(f32)
    mlp_v = np.concatenate([rs[c]['mvs'] for c in range(N_CORES)], 0).reshape(1, 128, 1, 2048).astype(f32)
    return (y_prompt, y_sample, ret_p, ret_s, mlp_v)
```
